# Optimizing a Trainium2 kernel written in Bass

```python
import math
import jax, jax.numpy as jnp
from jax import lax
import numpy as np

D_MODEL = 1024
BATCH = 4
SEQ = 4096
DEPTH = 1

CHUNK = 64
Q_BLOCK = 128
N_HEADS = 8
HEAD_DIM = 64
V_DIM = 2 * HEAD_DIM
QK_WIDTH = N_HEADS * 2 * HEAD_DIM
ATTN_WIDTH = N_HEADS * V_DIM
CONV_CH = D_MODEL
CONV_WIDTH = 31
D_FF = 4 * D_MODEL
ROPE_THETA = 10000.0
LN_EPS = 1e-5
DEEPNORM_ALPHA = (2.0 * DEPTH) ** 0.25
DEEPNORM_BETA = (8.0 * DEPTH) ** -0.25

SPLIT_SIZES = (QK_WIDTH, QK_WIDTH, ATTN_WIDTH, 2 * CONV_CH, 2 * D_MODEL)
IN_WIDTH = sum(SPLIT_SIZES)
SPLIT_POINTS = tuple(int(v) for v in np.cumsum(SPLIT_SIZES)[:-1])

kernel_name = "hybrid_diffattn_conformer_conv_gated"


def lambda_init_for(layer_idx):
    return 0.8 - 0.6 * math.exp(-0.3 * layer_idx)


def layer_norm(x, g, b):
    xf = x.astype(jnp.float32)
    mu = jnp.mean(xf, axis=-1, keepdims=True)
    var = jnp.mean(jnp.square(xf - mu), axis=-1, keepdims=True)
    y = (xf - mu) * lax.rsqrt(var + LN_EPS) * g.astype(jnp.float32) + b.astype(jnp.float32)
    return y.astype(x.dtype)


def rms_norm(x, g):
    xf = x.astype(jnp.float32)
    y = xf * lax.rsqrt(jnp.mean(jnp.square(xf), axis=-1, keepdims=True) + LN_EPS)
    return (y * g.astype(jnp.float32)).astype(x.dtype)


def apply_rope(t, cos, sin):
    half = HEAD_DIM // 2
    t1, t2 = t[..., :half], t[..., half:]
    return jnp.concatenate([t1 * cos - t2 * sin, t1 * sin + t2 * cos], axis=-1)


def chunk_causal_diff_attention(q, k, v, lam):
    seq = q.shape[3]
    scale = HEAD_DIM ** -0.5
    neg = jnp.finfo(jnp.float32).min
    outs = []
    for i in range(seq // Q_BLOCK):
        q0 = i * Q_BLOCK
        kend = q0 + Q_BLOCK
        qb = q[:, :, :, q0:kend]
        kb = k[:, :, :, :kend]
        s = jnp.einsum('bhcqd,bhckd->bhcqk', qb, kb).astype(jnp.float32) * scale
        q_chunk = (q0 + jnp.arange(Q_BLOCK)) // CHUNK
        k_chunk = jnp.arange(kend) // CHUNK
        mask = k_chunk[None, :] <= q_chunk[:, None]
        s = jnp.where(mask, s, neg)
        p = jax.nn.softmax(s, axis=-1)
        a = p[:, :, 0] - lam * p[:, :, 1]
        outs.append(jnp.einsum('bhqk,bhkv->bhqv', a.astype(v.dtype), v[:, :, :kend]))
    return jnp.concatenate(outs, axis=2)


def causal_depthwise_conv(u, kernel, bias):
    out = lax.conv_general_dilated(
        u, kernel[:, None, :].astype(u.dtype), window_strides=(1,),
        padding=((CONV_WIDTH - 1, 0),),
        dimension_numbers=('NWC', 'WIO', 'NWC'),
        feature_group_count=u.shape[-1])
    return out + bias


def setup_inputs(seed: int = 0) -> dict:
    key = jax.random.key(seed)
    ks = jax.random.split(key, 32)
    L, D = DEPTH, D_MODEL
    f32 = jnp.float32

    def nrm(k, shape, scale):
        return jax.random.normal(k, shape, f32) * scale

    x = jax.random.normal(ks[0], (BATCH, SEQ, D), f32)
    offset = jax.random.randint(ks[1], (BATCH, 1), 0, 1024, dtype=jnp.int32)
    positions = (offset + jnp.arange(SEQ, dtype=jnp.int32)[None, :]).astype(jnp.int32)

    s_in = D ** -0.5
    w_q = nrm(ks[2], (L, D, QK_WIDTH), s_in)
    w_k = nrm(ks[3], (L, D, QK_WIDTH), s_in)
    w_v = nrm(ks[4], (L, D, ATTN_WIDTH), s_in * DEEPNORM_BETA)
    w_glu = nrm(ks[5], (L, D, 2 * CONV_CH), s_in)
    w_gate = nrm(ks[6], (L, D, 2 * D), s_in)
    w_in = jnp.concatenate([w_q, w_k, w_v, w_glu, w_gate], axis=-1)

    return {
        "x": x,
        "positions": positions,
        "w_in": w_in,
        "b_glu": nrm(ks[7], (L, 2 * CONV_CH), 0.02),
        "b_gate": nrm(ks[8], (L, 2 * D), 0.02),
        "lambda_q1": nrm(ks[9], (L, HEAD_DIM), 0.1),
        "lambda_k1": nrm(ks[10], (L, HEAD_DIM), 0.1),
        "lambda_q2": nrm(ks[11], (L, HEAD_DIM), 0.1),
        "lambda_k2": nrm(ks[12], (L, HEAD_DIM), 0.1),
        "subln_g": 1.0 + nrm(ks[13], (L, V_DIM), 0.02),
        "dw_kernel": nrm(ks[14], (L, CONV_WIDTH, CONV_CH), CONV_WIDTH ** -0.5),
        "dw_bias": nrm(ks[15], (L, CONV_CH), 0.02),
        "conv_ln_g": 1.0 + nrm(ks[16], (L, CONV_CH), 0.02),
        "conv_ln_b": nrm(ks[17], (L, CONV_CH), 0.02),
        "w_pw2": nrm(ks[18], (L, CONV_CH, D), CONV_CH ** -0.5 * DEEPNORM_BETA),
        "b_pw2": nrm(ks[19], (L, D), 0.02),
        "w_out": nrm(ks[20], (L, D, D), D ** -0.5 * DEEPNORM_BETA),
        "ln1_g": 1.0 + nrm(ks[21], (L, D), 0.02),
        "ln1_b": nrm(ks[22], (L, D), 0.02),
        "w_ff1": nrm(ks[23], (L, D, D_FF), D ** -0.5),
        "w_ff2": nrm(ks[24], (L, D_FF, D), D_FF ** -0.5 * DEEPNORM_BETA),
        "ln2_g": 1.0 + nrm(ks[25], (L, D), 0.02),
        "ln2_b": nrm(ks[26], (L, D), 0.02),
    }


def reference(x, positions, w_in, b_glu, b_gate, lambda_q1, lambda_k1, lambda_q2,
              lambda_k2, subln_g, dw_kernel, dw_bias, conv_ln_g, conv_ln_b, w_pw2,
              b_pw2, w_out, ln1_g, ln1_b, w_ff1, w_ff2, ln2_g, ln2_b):
    B, S, _ = x.shape
    half = HEAD_DIM // 2
    inv_freq = ROPE_THETA ** (-jnp.arange(half, dtype=jnp.float32) * 2.0 / HEAD_DIM)
    ang = positions.astype(jnp.float32)[..., None] * inv_freq
    cos = jnp.cos(ang)[:, :, None, None, :].astype(x.dtype)
    sin = jnp.sin(ang)[:, :, None, None, :].astype(x.dtype)

    h = x
    for l in range(DEPTH):
        lam_init = lambda_init_for(l)
        proj = jnp.einsum('bsd,de->bse', h, w_in[l])
        q, k, v, glu, gates = jnp.split(proj, SPLIT_POINTS, axis=-1)

        q = apply_rope(q.reshape(B, S, N_HEADS, 2, HEAD_DIM), cos, sin)
        k = apply_rope(k.reshape(B, S, N_HEADS, 2, HEAD_DIM), cos, sin)
        q = jnp.transpose(q, (0, 2, 3, 1, 4))
        k = jnp.transpose(k, (0, 2, 3, 1, 4))
        v = jnp.transpose(v.reshape(B, S, N_HEADS, V_DIM), (0, 2, 1, 3))
        lam = (jnp.exp(jnp.sum(lambda_q1[l].astype(jnp.float32) * lambda_k1[l].astype(jnp.float32)))
               - jnp.exp(jnp.sum(lambda_q2[l].astype(jnp.float32) * lambda_k2[l].astype(jnp.float32)))
               + lam_init)
        att = chunk_causal_diff_attention(q, k, v, lam)
        att = rms_norm(att, subln_g[l]) * (1.0 - lam_init)
        att = jnp.transpose(att, (0, 2, 1, 3)).reshape(B, S, ATTN_WIDTH)

        glu = glu + b_glu[l]
        ga, gb = jnp.split(glu, 2, axis=-1)
        u = ga * jax.nn.sigmoid(gb)
        u = causal_depthwise_conv(u, dw_kernel[l], dw_bias[l])
        u = jax.nn.silu(layer_norm(u, conv_ln_g[l], conv_ln_b[l]))
        conv = jnp.einsum('bsc,cd->bsd', u, w_pw2[l]) + b_pw2[l]

        g = jax.nn.sigmoid(gates + b_gate[l])
        g_att, g_conv = jnp.split(g, 2, axis=-1)
        mixed = jnp.einsum('bsd,de->bse', g_att * att + g_conv * conv, w_out[l])
        h = layer_norm(DEEPNORM_ALPHA * h + mixed, ln1_g[l], ln1_b[l])

        ff = jnp.square(jax.nn.relu(jnp.einsum('bsd,df->bsf', h, w_ff1[l])))
        ff = jnp.einsum('bsf,fd->bsd', ff, w_ff2[l])
        h = layer_norm(DEEPNORM_ALPHA * h + ff, ln2_g[l], ln2_b[l])
    return h
```

```python
import math
import numpy as np
from contextlib import ExitStack
import concourse.bass as bass
import concourse.mybir as mybir
from concourse.bass_utils import run_bass_kernel_spmd

F32 = mybir.dt.float32
BF16 = mybir.dt.bfloat16
I32 = mybir.dt.int32
AF = mybir.ActivationFunctionType
ALU = mybir.AluOpType
AX = mybir.AxisListType

D = 1024
KC = 8
SEQ = 4096
NB = 32
NO = 16
H = 8
TOK = 2048
LN_EPS = 1e-5
ALPHA = 2.0 ** 0.25
LAM_INIT = 0.8 - 0.6 * math.exp(0.0)
NPV = 312
NPR = 4480
PI = math.pi


class Eng:
    def __init__(self, nc, eng, name, is_pe=False):
        self.e = eng
        self.name = name
        self.sem = nc.alloc_semaphore("s_" + name)
        self.cnt = 0
        self.seen = {}
        self.is_pe = is_pe

    def wait(self, deps):
        for src, val in deps:
            if src is self and self.is_pe:
                continue
            if self.seen.get(src, 0) >= val:
                continue
            self.e.wait_ge(src.sem, val)
            self.seen[src] = val

    def sig(self, ins):
        ins.then_inc(self.sem, 1)
        self.cnt += 1
        return (self, self.cnt)


class Dq:
    registry = []

    def __init__(self, nc, name):
        self.sem = nc.alloc_semaphore("d_" + name)
        self.cnt = 0
        Dq.registry.append(self)


class Res:
    def __init__(self):
        self.w = []
        self.r = {}


def _deps(reads, writes, extra):
    deps = list(extra)
    for b in reads:
        deps += b.w
    for b in writes:
        deps += b.w
        deps += list(b.r.items())
    return deps


def _upd(t, reads, writes):
    for b in reads:
        b.r[t[0]] = t[1]
    for b in writes:
        b.w = [t]
        b.r = {}


def do(eng, fn, reads=(), writes=(), extra=()):
    eng.wait(_deps(reads, writes, extra))
    t = eng.sig(fn(eng.e))
    _upd(t, reads, writes)
    return t


def group(eng, fns, reads=(), writes=(), extra=()):
    eng.wait(_deps(reads, writes, extra))
    ins = None
    for fn in fns:
        ins = fn(eng.e)
    t = eng.sig(ins)
    _upd(t, reads, writes)
    return t


def dma(queue, dq, out, in_, reads=(), writes=(), extra=()):
    queue.wait(_deps(reads, writes, extra))
    ins = queue.e.dma_start(out=out, in_=in_)
    ins.then_inc(dq.sem, 16)
    dq.cnt += 16
    t = (dq, dq.cnt)
    _upd(t, reads, writes)
    return t


def build_program(stop_after=None, skip_b=False, max_j=NO):
    nc = bass.Bass("TRN2", target_bir_lowering=False)
    Dq.registry = []
    dt = nc.dram_tensor
    xT = dt("xT", [D, SEQ], F32, kind="ExternalInput").ap()
    xhT = dt("xhT", [D, 512], F32, kind="ExternalInput").ap()
    xown = dt("xown", [TOK, D], F32, kind="ExternalInput").ap()
    pos_tm = dt("pos_tm", [128, NB], I32, kind="ExternalInput").ap()
    pos_blk = dt("pos_blk", [1, NB], I32, kind="ExternalInput").ap()
    w_in = dt("w_in", [D, 7168], F32, kind="ExternalInput").ap()
    w_pw2 = dt("w_pw2", [D, D], F32, kind="ExternalInput").ap()
    w_out = dt("w_out", [D, D], F32, kind="ExternalInput").ap()
    w_ff1 = dt("w_ff1", [D, 4096], F32, kind="ExternalInput").ap()
    w_ff2 = dt("w_ff2", [4096, D], F32, kind="ExternalInput").ap()
    pvec = dt("pvec", [128, NPV], F32, kind="ExternalInput").ap()
    prow = dt("prow", [1, NPR], F32, kind="ExternalInput").ap()
    ident_in = dt("ident", [128, 128], F32, kind="ExternalInput").ap()
    y = dt("y", [TOK, D], F32, kind="ExternalOutput").ap()
    dbg = None
    if stop_after is not None:
        dbg = dt("dbg", [128, 16384], F32, kind="ExternalOutput").ap()

    es = ExitStack()

    def sb(name, shape, dtype):
        return es.enter_context(nc.sbuf_tensor(name, shape, dtype))

    PE = Eng(nc, nc.tensor, "pe", is_pe=True)
    ACT = Eng(nc, nc.scalar, "act")
    DVE = Eng(nc, nc.vector, "dve")
    POOL = Eng(nc, nc.gpsimd, "pool")
    SP = Eng(nc, nc.sync, "sp")

    P8 = es.enter_context(nc.psum_tensor("P8", [128, 8, 512], F32))
    bankR = [Res() for _ in range(8)]

    cm = sb("cm", [128, KC, TOK], BF16)
    pv = sb("pv", [128, NPV], F32)
    ident_f = sb("ident_f", [128, 128], F32)
    ident_b = sb("ident_b", [128, 128], BF16)
    ones_b = sb("ones_b", [128, 128], BF16)
    cst = sb("cst", [128, 8], F32)
    posi = sb("posi", [128, NB], I32)
    posbi = sb("posbi", [128, NB], I32)
    posf = sb("posf", [128, NB], F32)
    posbf = sb("posbf", [128, NB], F32)
    invf = sb("invf", [128, 32], F32)
    mb = sb("mb", [128, NO], F32)
    hv = sb("hv", [128, NO], F32)
    lamc = sb("lamc", [128, 8], F32)
    sg08 = sb("sg08", [128, 128], F32)
    esX = ExitStack()
    xT_bf = esX.enter_context(nc.sbuf_tensor("xT_bf", [128, KC, SEQ], BF16))
    cosT = esX.enter_context(nc.sbuf_tensor("cosT", [128, NB, 32], F32))
    sinT = esX.enter_context(nc.sbuf_tensor("sinT", [128, NB, 32], F32))

    xTR = Res()
    cmR = [[Res() for _ in range(4)] for _ in range(KC)]
    constR = Res()

    esB = ExitStack()
    if skip_b:
        for c_ in range(KC):
            do(POOL, lambda e, c_=c_: e.memset(cm[:, c_, :], 0.0), writes=cmR[c_])

    def sbB(name, shape, dtype):
        return esB.enter_context(nc.sbuf_tensor(name, shape, dtype))

    xh_bf = sbB("xh_bf", [128, KC, 512], BF16)
    xhR = Res()
    dq_xh = Dq(nc, "xh")
    dma(POOL, dq_xh, xh_bf[:, :, :], xhT.rearrange("(k p) n -> p k n", p=128), writes=[xhR])

    esB1 = ExitStack()
    wgl = esB1.enter_context(nc.sbuf_tensor("wgl", [128, KC, 2048], BF16))
    wglR = [Res() for _ in range(4)]
    dq_wgl = [Dq(nc, "wgl%d" % i) for i in range(4)]
    def load_wgl(g4):
        dma(POOL, dq_wgl[g4], wgl[:, :, g4 * 512:(g4 + 1) * 512],
            w_in[:, 3072 + g4 * 512:3072 + (g4 + 1) * 512].rearrange("(k p) n -> p k n", p=128), writes=[wglR[g4]])

    load_wgl(0)
    load_wgl(2)
    diag = [esB1.enter_context(nc.sbuf_tensor("diag%d" % i, [128, 31, 128], BF16)) for i in range(2)]
    diagR = [Res(), Res()]
    ubuf = [esB1.enter_context(nc.sbuf_tensor("ubuf%d" % i, [128, NO, 160], BF16)) for i in range(2)]
    ubufR = [Res(), Res()]
    sgb = [esB1.enter_context(nc.sbuf_tensor("sgb%d" % i, [128, 512], F32)) for i in range(2)]
    sgbR = [Res(), Res()]
    xT_blk = [xT_bf[:, k, :].rearrange("p (b t) -> p b t", t=128) for k in range(KC)]

    esP = ExitStack()
    pr0 = esP.enter_context(nc.sbuf_tensor("pr0", [128, 384], F32))
    dq_init = Dq(nc, "init")
    dq_x = Dq(nc, "x")

    t_pv = dma(SP, dq_init, pv[:, :], pvec[:, :])
    dma(SP, dq_init, pr0[:, 0:128], prow[0:1, 0:128].broadcast_to([128, 128]))
    dma(SP, dq_init, pr0[:, 128:384], prow[0:1, 4224:4480].broadcast_to([128, 256]))
    dma(SP, dq_init, posi[:, :], pos_tm[:, :])
    dma(SP, dq_init, posbi[:, :], pos_blk[0:1, :].broadcast_to([128, NB]))
    t_init = dma(SP, dq_init, ident_f[:, :], ident_in[:, :])
    xTRh = [Res(), Res()]
    dq_xb = Dq(nc, "xb")
    for hf, dqx in ((0, dq_x), (1, dq_xb)):
        for k in range(KC):
            t_x = dma(POOL, dqx, xT_bf[:, k, hf * 2048:(hf + 1) * 2048],
                      xT[k * 128:(k + 1) * 128, hf * 2048:(hf + 1) * 2048])
        xTRh[hf].w = [t_x]
    xTR.w = xTRh[0].w + xTRh[1].w
    load_wgl(1)
    load_wgl(3)

    do(POOL, lambda e: e.memset(cst[:, 0:1], 0.0))
    do(POOL, lambda e: e.memset(cst[:, 1:2], -PI))
    do(POOL, lambda e: e.memset(cst[:, 2:3], LN_EPS))
    do(POOL, lambda e: e.memset(ones_b[:, :], 1.0 / 1024.0))
    for i in range(32):
        val = float(np.float32(10000.0) ** np.float32(-(2.0 * i) / 64.0))
        t_c = do(POOL, lambda e, i=i, val=val: e.memset(invf[:, i:i + 1], val))
    constR.w = [t_c]

    ini = [t_init]
    do(DVE, lambda e: e.tensor_copy(out=ident_b[:, :], in_=ident_f[:, :]), extra=ini, writes=[constR])
    do(DVE, lambda e: e.tensor_copy(out=posf[:, :], in_=posi[:, :]), writes=[constR])
    do(DVE, lambda e: e.tensor_copy(out=posbf[:, :], in_=posbi[:, :]), writes=[constR])
    es0 = ExitStack()

    def sb0(name, shape, dtype):
        return es0.enter_context(nc.sbuf_tensor(name, shape, dtype))

    ang = sb0("ang", [128, NB, 32], F32)
    ang2 = sb0("ang2", [128, NB, 32], F32)
    angn = sb0("angn", [128, NB, 32], F32)
    angi = sb0("angi", [128, NB, 32], I32)
    angR = Res()
    ang2R = Res()
    tabR = Res()
    C1 = 6.28125
    C2 = 2 * PI - C1
    do(DVE, lambda e: e.tensor_tensor(out=ang[:, :, :], in0=posf[:, :].unsqueeze(2).broadcast_to([128, NB, 32]),
                                      in1=invf[:, :].unsqueeze(1).to_broadcast([128, NB, 32]), op=ALU.mult),
       reads=[constR], writes=[angR])

    def range_reduced_sin(dst):
        do(DVE, lambda e: e.tensor_scalar(out=angi[:, :, :], in0=ang[:, :, :], scalar1=1.0 / (2 * PI), scalar2=None,
                                          op0=ALU.mult), reads=[angR], writes=[ang2R])
        do(DVE, lambda e: e.tensor_copy(out=angn[:, :, :], in_=angi[:, :, :]), reads=[ang2R], writes=[ang2R])
        do(DVE, lambda e: e.scalar_tensor_tensor(out=ang2[:, :, :], in0=angn[:, :, :], scalar=-C1, in1=ang[:, :, :],
                                                 op0=ALU.mult, op1=ALU.add), reads=[ang2R, angR], writes=[ang2R])
        do(DVE, lambda e: e.scalar_tensor_tensor(out=ang2[:, :, :], in0=angn[:, :, :], scalar=-C2, in1=ang2[:, :, :],
                                                 op0=ALU.mult, op1=ALU.add), reads=[ang2R], writes=[ang2R])
        do(DVE, lambda e: e.tensor_scalar(out=ang2[:, :, :], in0=ang2[:, :, :], scalar1=-3.141592, scalar2=3.141592,
                                          op0=ALU.max, op1=ALU.min), reads=[ang2R], writes=[ang2R])
        do(ACT, lambda e: e.activation(out=dst, in_=ang2[:, :, :], func=AF.Sin, bias=cst[:, 0:1], scale=1.0),
           reads=[ang2R, constR], writes=[tabR])

    range_reduced_sin(sinT[:, :, :])
    do(DVE, lambda e: e.tensor_scalar(out=ang[:, :, :], in0=ang[:, :, :], scalar1=0.5 * PI, scalar2=None,
                                      op0=ALU.add), reads=[angR, ang2R], writes=[angR])
    range_reduced_sin(cosT[:, :, :])

    lq = sb0("lq", [128, 128], F32)
    lqR = Res()
    do(DVE, lambda e: e.tensor_tensor(out=lq[:, 0:64], in0=pr0[:, 128:192], in1=pr0[:, 192:256], op=ALU.mult),
       writes=[lqR])
    do(DVE, lambda e: e.tensor_tensor(out=lq[:, 64:128], in0=pr0[:, 256:320], in1=pr0[:, 320:384], op=ALU.mult),
       writes=[lqR])
    lamR = Res()
    do(DVE, lambda e: e.reduce_sum(out=lamc[:, 0:1], in_=lq[:, 0:64], axis=AX.X), reads=[lqR], writes=[lamR])
    do(DVE, lambda e: e.reduce_sum(out=lamc[:, 1:2], in_=lq[:, 64:128], axis=AX.X), reads=[lqR], writes=[lamR])
    do(ACT, lambda e: e.activation(out=lamc[:, 2:4], in_=lamc[:, 0:2], func=AF.Exp, bias=cst[:, 0:1], scale=1.0),
       reads=[lamR, constR], writes=[lamR])
    do(DVE, lambda e: e.tensor_tensor(out=lamc[:, 4:5], in0=lamc[:, 2:3], in1=lamc[:, 3:4], op=ALU.subtract),
       reads=[lamR], writes=[lamR])
    do(DVE, lambda e: e.tensor_scalar(out=lamc[:, 5:6], in0=lamc[:, 4:5], scalar1=-1.0, scalar2=-LAM_INIT,
                                      op0=ALU.mult, op1=ALU.add), reads=[lamR], writes=[lamR])
    do(DVE, lambda e: e.tensor_scalar(out=sg08[:, :], in0=pr0[:, 0:128], scalar1=1.0 - LAM_INIT, scalar2=None,
                                      op0=ALU.mult), writes=[constR])
    pb2 = posbf[:, :].rearrange("p (j t) -> p j t", t=2)
    do(DVE, lambda e: e.tensor_tensor(out=mb[:, :], in0=pb2[:, :, 0], in1=pb2[:, :, 1], op=ALU.is_gt),
       reads=[constR], writes=[constR])
    do(DVE, lambda e: e.tensor_scalar(out=mb[:, :], in0=mb[:, :], scalar1=-30000.0, scalar2=None, op0=ALU.mult),
       reads=[constR], writes=[constR])
    do(DVE, lambda e: e.tensor_reduce(out=lamc[:, 6:7], in_=posbf[:, :], axis=AX.X, op=ALU.min),
       reads=[constR], writes=[lamR])
    do(DVE, lambda e: e.tensor_scalar(out=hv[:, :], in0=pb2[:, :, 1], scalar1=lamc[:, 6:7], scalar2=None,
                                      op0=ALU.is_gt), reads=[lamR, constR], writes=[constR])

    all_dq = Dq.registry

    def barrier(exclude=()):
        engs = [PE, ACT, DVE, POOL, SP]
        tk = [(e_, e_.cnt) for e_ in engs if e_.cnt > 0] + \
             [(q, q.cnt) for q in all_dq if q.cnt > 0 and q not in exclude]
        for e_ in engs:
            e_.wait([t_ for t_ in tk if t_[0] is not e_])

    es0.close()
    esP.close()

    def finish(src_ap, ncols):
        dq = Dq(nc, "dbg")
        for c0 in range(0, ncols, 2048):
            c1 = min(ncols, c0 + 2048)
            t = dma(SP, dq, dbg[:, c0:c1], src_ap[:, c0:c1],
                    extra=[(DVE, DVE.cnt), (ACT, ACT.cnt), (POOL, POOL.cnt), (PE, PE.cnt)])
        SP.wait([t])
        return nc

    if stop_after == "p0":
        d0 = sb("d0", [128, 4096], F32)
        do(DVE, lambda e: e.tensor_copy(out=d0[:, 0:1024], in_=cosT[:, :, :].rearrange("p a b -> p (a b)")), reads=[tabR])
        do(DVE, lambda e: e.tensor_copy(out=d0[:, 1024:2048], in_=sinT[:, :, :].rearrange("p a b -> p (a b)")), reads=[tabR])
        do(DVE, lambda e: e.tensor_copy(out=d0[:, 2048:2064], in_=mb[:, :]), reads=[constR])
        do(DVE, lambda e: e.tensor_copy(out=d0[:, 2064:2080], in_=hv[:, :]), reads=[constR])
        do(DVE, lambda e: e.tensor_copy(out=d0[:, 2080:2088], in_=lamc[:, :]), reads=[lamR])
        do(DVE, lambda e: e.tensor_copy(out=d0[:, 2088:2088 + 128], in_=xT_bf[:, 3, 1000:1128]), reads=[xTR])
        return finish(d0[:, 0:4096], 4096)

    gcount = 0
    ycount = 0
    for c in ([] if skip_b else range(KC)):
        dg = diag[c % 2]
        dgR = diagR[c % 2]
        for k in range(31):
            do(DVE, lambda e, k=k, c=c, dg=dg: e.tensor_scalar(
                out=dg[:, k, :], in0=ident_f[:, :], scalar1=pv[:, 64 + c * 31 + k:64 + c * 31 + k + 1], scalar2=None,
                op0=ALU.mult), writes=[dgR], extra=[t_pv, t_init])
        ub = ubuf[c % 2]
        ubR = ubufR[c % 2]
        for grp in range(5):
            pa = gcount % 2
            gcount += 1
            ga_ps = P8[:, pa, :]
            gb_ps = P8[:, 2 + pa, :]
            if grp == 0:
                rhs = [xh_bf[:, k, :] for k in range(KC)]
                rr = [xhR]
            else:
                tg = grp - 1
                rhs = [xT_blk[k][:, 8 * tg + 1:8 * tg + 8:2, :] for k in range(KC)]
                rr = [xTRh[tg // 2]]
            wr = wglR[c // 4]
            wrb = wglR[2 + c // 4]
            fa = [lambda e, k=k: e.matmul(ga_ps if grp == 0 else ga_ps.rearrange("p (b t) -> p b t", t=128),
                                          wgl[:, k, c * 128:(c + 1) * 128], rhs[k], start=(k == 0), stop=(k == KC - 1))
                  for k in range(KC)]
            group(PE, fa, reads=rr + [wr], writes=[bankR[pa]])
            fb = [lambda e, k=k: e.matmul(gb_ps if grp == 0 else gb_ps.rearrange("p (b t) -> p b t", t=128),
                                          wgl[:, k, 1024 + c * 128:1024 + (c + 1) * 128], rhs[k],
                                          start=(k == 0), stop=(k == KC - 1)) for k in range(KC)]
            group(PE, fb, reads=rr + [wrb], writes=[bankR[2 + pa]])
            sg = sgb[pa]
            do(ACT, lambda e: e.activation(out=sg[:, :], in_=gb_ps, func=AF.Sigmoid, bias=pv[:, 8 + c:9 + c], scale=1.0),
               reads=[bankR[2 + pa]], writes=[sgbR[pa]], extra=[t_pv])
            if grp == 0:
                o_ap = ub[:, :, 0:32]
                i0 = ga_ps.rearrange("p (b t) -> p b t", t=32)
                i1 = sg[:, :].rearrange("p (b t) -> p b t", t=32)
            else:
                o_ap = ub[:, 4 * tg:4 * tg + 4, 32:160]
                i0 = ga_ps.rearrange("p (b t) -> p b t", t=128)
                i1 = sg[:, :].rearrange("p (b t) -> p b t", t=128)
            do(DVE, lambda e: e.scalar_tensor_tensor(out=o_ap, in0=i0, scalar=pv[:, c:c + 1], in1=i1,
                                                     op0=ALU.add, op1=ALU.mult),
               reads=[bankR[pa], sgbR[pa]], writes=[ubR], extra=[t_pv])
            if grp == 0:
                do(DVE, lambda e: e.tensor_tensor(out=ub[:, :, 0:32], in0=ub[:, :, 0:32],
                                                  in1=hv[:, :].unsqueeze(2).broadcast_to([128, NO, 32]), op=ALU.mult),
                   reads=[constR], writes=[ubR])
        for tg in range(4):
            yb = 4 + (ycount % 2)
            ycount += 1
            y_ps = P8[:, yb, :]
            fc = [lambda e, k=k: e.matmul(y_ps.rearrange("p (b t) -> p b t", t=128), dg[:, k, :],
                                          ub[:, 4 * tg:4 * tg + 4, 2 + k:2 + k + 128], start=(k == 0), stop=(k == 30))
                  for k in range(31)]
            group(PE, fc, reads=[dgR, ubR], writes=[bankR[yb]])
            do(ACT, lambda e: e.activation(out=cm[:, c, tg * 512:(tg + 1) * 512], in_=y_ps, func=AF.Identity,
                                           bias=pv[:, 32 + c:33 + c], scale=1.0),
               reads=[bankR[yb]], writes=[cmR[c][tg]], extra=[t_pv])
    barrier()
    esB1.close()

    if stop_after == "b1":
        d0 = sb("d0", [128, 16384], F32)
        for c in range(KC):
            do(DVE, lambda e: e.tensor_copy(out=d0[:, c * 2048:(c + 1) * 2048], in_=cm[:, c, :]), reads=cmR[c])
        return finish(d0[:, :], 16384)

    esB2 = ExitStack()

    def sbB2(name, shape, dtype):
        return esB2.enter_context(nc.sbuf_tensor(name, shape, dtype))

    wpw = sbB2("wpw", [128, KC, D], BF16)
    wpwR = Res()
    dq_wpw = Dq(nc, "wpw")
    for hf in range(2):
        t_w = dma(POOL, dq_wpw, wpw[:, :, hf * 512:(hf + 1) * 512],
                  w_pw2[:, hf * 512:(hf + 1) * 512].rearrange("(k p) n -> p k n", p=128))
    wpwR.w = [t_w]
    ysq = [sbB2("ysq%d" % i, [128, 512], BF16) for i in range(2)]
    ysqR = [Res(), Res()]
    mean = [sbB2("mean%d" % i, [128, 512], F32) for i in range(2)]
    msq = [sbB2("msq%d" % i, [128, 512], F32) for i in range(2)]
    rstd = [sbB2("rstd%d" % i, [128, 512], F32) for i in range(2)]
    statR = [Res(), Res()]
    t1 = [sbB2("t1_%d" % i, [128, 512], F32) for i in range(2)]
    t1R = [Res(), Res()]
    zT = [sbB2("zT%d" % i, [128, KC, 512], BF16) for i in range(2)]
    zTR = [Res(), Res()]
    ocount = [0]
    S1 = P8[:, 6, :]
    S2 = P8[:, 7, :]

    def b2_stats(tg):
        sl = slice(tg * 512, (tg + 1) * 512)
        PE.wait(_deps([], [bankR[6], bankR[7]], []))
        for c in range(KC):
            q = ysq[c % 2]
            do(POOL, lambda e: e.tensor_tensor(out=q[:, :], in0=cm[:, c, sl], in1=cm[:, c, sl], op=ALU.mult),
               reads=[cmR[c][tg]], writes=[ysqR[c % 2]])
            group(PE, [lambda e: e.matmul(S1, ones_b[:, :], cm[:, c, sl], start=(c == 0), stop=(c == KC - 1))],
                  reads=[cmR[c][tg], constR])
            t_s = group(PE, [lambda e: e.matmul(S2, ones_b[:, :], q[:, :], start=(c == 0), stop=(c == KC - 1))],
                        reads=[ysqR[c % 2]])
        for bk in (6, 7):
            bankR[bk].w = [t_s]
            bankR[bk].r = {}

    def b2_norm_a(tg):
        i = tg % 2
        do(DVE, lambda e: e.tensor_copy(out=mean[i][:, :], in_=S1), reads=[bankR[6]], writes=[statR[i]])
        do(DVE, lambda e: e.tensor_tensor(out=msq[i][:, :], in0=S1, in1=mean[i][:, :], op=ALU.mult),
           reads=[bankR[6], statR[i]], writes=[statR[i]])
        do(DVE, lambda e: e.tensor_tensor(out=msq[i][:, :], in0=S2, in1=msq[i][:, :], op=ALU.subtract),
           reads=[bankR[7], statR[i]], writes=[statR[i]])

    def b2_norm_b(tg):
        i = tg % 2
        sl = slice(tg * 512, (tg + 1) * 512)
        do(ACT, lambda e: e.activation(out=msq[i][:, :], in_=msq[i][:, :], func=AF.Sqrt, bias=cst[:, 2:3], scale=1.0),
           reads=[statR[i], constR], writes=[statR[i]])
        do(DVE, lambda e: e.reciprocal(out=rstd[i][:, :], in_=msq[i][:, :]), reads=[statR[i]], writes=[statR[i]])
        z = zT[tg % 2]
        zR = zTR[tg % 2]
        for c in range(KC):
            tt = t1[c % 2]
            do(DVE, lambda e: e.tensor_tensor(out=tt[:, :], in0=cm[:, c, sl], in1=mean[i][:, :], op=ALU.subtract),
               reads=[cmR[c][tg], statR[i]], writes=[t1R[c % 2]])
            do(DVE, lambda e: e.tensor_tensor(out=tt[:, :], in0=tt[:, :], in1=rstd[i][:, :], op=ALU.mult),
               reads=[statR[i]], writes=[t1R[c % 2]])
            do(ACT, lambda e: e.activation(out=z[:, c, :], in_=tt[:, :], func=AF.Silu, bias=pv[:, 48 + c:49 + c],
                                           scale=pv[:, 40 + c:41 + c]), reads=[t1R[c % 2]], writes=[zR])

    def b2_pw2(tg):
        sl = slice(tg * 512, (tg + 1) * 512)
        z = zT[tg % 2]
        zR = zTR[tg % 2]
        for dc in range(KC):
            ob = ocount[0] % 2
            ocount[0] += 1
            o_ps = P8[:, ob, :]
            fo = [lambda e, c=c: e.matmul(o_ps, wpw[:, c, dc * 128:(dc + 1) * 128], z[:, c, :],
                                          start=(c == 0), stop=(c == KC - 1)) for c in range(KC)]
            group(PE, fo, reads=[wpwR, zR], writes=[bankR[ob]])
            do(ACT, lambda e: e.activation(out=cm[:, dc, sl], in_=o_ps, func=AF.Identity, bias=pv[:, 56 + dc:57 + dc],
                                           scale=1.0), reads=[bankR[ob]], writes=[cmR[dc][tg]])

    if not skip_b:
        b2_stats(0)
        b2_norm_a(0)
        b2_stats(1)
        b2_norm_b(0)
        b2_norm_a(1)
        for tg in range(4):
            if tg + 2 < 4:
                b2_stats(tg + 2)
            if tg + 1 < 4:
                b2_norm_b(tg + 1)
            if tg + 2 < 4:
                b2_norm_a(tg + 2)
            b2_pw2(tg)
    barrier()
    esB2.close()
    esB.close()

    if stop_after == "b2":
        d0 = sb("d0", [128, 16384], F32)
        for c in range(KC):
            do(DVE, lambda e: e.tensor_copy(out=d0[:, c * 2048:(c + 1) * 2048], in_=cm[:, c, :]), reads=cmR[c])
        return finish(d0[:, :], 16384)

    esA = ExitStack()

    def sbA(name, shape, dtype):
        return esA.enter_context(nc.sbuf_tensor(name, shape, dtype))

    wqkv = [sbA("wqkv%d" % i, [128, KC, 384], BF16) for i in range(2)]
    wqkvR = [Res(), Res()]
    dq_wqkv = [Dq(nc, "wqkv%d" % i) for i in range(2)]
    wg = sbA("wg", [128, KC, 256], BF16)
    wgR = Res()
    dq_wg = Dq(nc, "wg")
    KT = [sbA("KT%d" % i, [128, SEQ], BF16) for i in range(2)]
    KTR = [[Res() for _ in range(NB)] for _ in range(2)]
    QT = [sbA("QT%d" % i, [128, TOK], BF16) for i in range(2)]
    QTR = [[Res() for _ in range(NO)] for _ in range(2)]
    VX = [sbA("VX%d" % i, [128, NB, 129], BF16) for i in range(2)]
    VXR = [[Res() for _ in range(NB)] for _ in range(2)]
    attT = sbA("attT", [128, TOK], BF16)
    attTR = [Res() for _ in range(NO)]
    PT = [sbA("PT%d" % i, [128, 2, 512], BF16) for i in range(3)]
    PTR = [Res() for _ in range(3)]
    PTd = [sbA("PTd%d" % i, [128, 2, 256], BF16) for i in range(2)]
    PTdR = [Res() for _ in range(2)]
    NRB = 4
    ra = [sbA("ra%d" % i, [128, 256], F32) for i in range(NRB)]
    rb = [sbA("rb%d" % i, [128, 256], F32) for i in range(NRB)]
    rabR = [Res() for _ in range(NRB)]
    qk_tm = [sbA("qk_tm%d" % i, [128, 256], BF16) for i in range(NRB)]
    qkR = [Res() for _ in range(NRB)]
    Oc = [sbA("Oc%d" % i, [128, 2, 132], F32) for i in range(2)]
    OcR = [Res(), Res()]
    nst = [sbA("nst%d" % i, [128, 8], F32) for i in range(2)]
    nstR = [Res(), Res()]
    dd = [sbA("dd%d" % i, [128, 128], F32) for i in range(2)]
    ddR = [Res(), Res()]
    A16 = sbA("A16", [128, NO, 128], F32)
    A16R = [Res() for _ in range(NO)]
    ss16 = sbA("ss16", [128, 3, NO], F32)
    ss16R = Res()
    junk = sbA("junk", [128, 128], F32)
    att16 = sbA("att16", [128, NO, 128], BF16)
    att16R = [Res() for _ in range(NO)]
    sA = sbA("sA", [128, 512], F32)
    sC = sbA("sC", [128, 512], F32)
    sAR = Res()
    sCR = Res()
    misc_bf = P8[:, 7, :].bitcast(BF16)

    for i in range(2):
        do(POOL, lambda e, i=i: e.memset(VX[i][:, :, 128:129], 1.0), writes=VXR[i])
        do(POOL, lambda e, i=i: e.memset(PTd[i][:, :, :], 0.0), writes=[PTdR[i]])

    def load_wqkv(h):
        hb = h % 2
        for i, base in enumerate((0, 1024, 2048)):
            t = dma(POOL, dq_wqkv[hb], wqkv[hb][:, :, i * 128:(i + 1) * 128],
                    w_in[:, base + h * 128:base + (h + 1) * 128].rearrange("(k p) n -> p k n", p=128),
                    writes=[wqkvR[hb]] if i == 0 else [])
        wqkvR[hb].w = [t]

    def load_wg(h):
        for i, base in enumerate((5120, 6144)):
            t = dma(POOL, dq_wg, wg[:, :, i * 128:(i + 1) * 128],
                    w_in[:, base + h * 128:base + (h + 1) * 128].rearrange("(k p) n -> p k n", p=128),
                    writes=[wgR] if i == 0 else [])
        wgR.w = [t]

    pcount = [0]
    proj_pending = []

    def flush_proj(keep):
        while len(proj_pending) > keep:
            proj_pending.pop(0)()

    def proj_block(h, tb, pbank=6):
        hb = h % 2
        w = wqkv[hb]
        own = (tb % 2 == 1)
        j = tb // 2
        c0 = 0 if own else 128
        pb = pcount[0] % NRB
        pcount[0] += 1
        pps = P8[:, pbank, :]
        group(PE, [lambda e, k=k: e.matmul(pps[:, c0:384], xT_bf[:, k, tb * 128:(tb + 1) * 128], w[:, k, c0:384],
                                           start=(k == 0), stop=(k == KC - 1)) for k in range(KC)],
              reads=[xTR, wqkvR[hb]], writes=[bankR[pbank]])
        do(DVE, lambda e: e.tensor_copy(out=VX[hb][:, tb, 0:128], in_=pps[:, 256:384]),
           reads=[bankR[pbank]], writes=[VXR[hb][tb]])
        nco = 4 if own else 2
        cs = slice(0, 4) if own else slice(2, 4)
        T = pps[:, c0:256].rearrange("p (c t d) -> p c t d", c=nco, t=2)
        cosb = cosT[:, tb, :].unsqueeze(1).unsqueeze(1).to_broadcast([128, nco, 2, 32])
        sinb = sinT[:, tb, :].unsqueeze(1).unsqueeze(1).to_broadcast([128, nco, 2, 32])
        RA = ra[pb][:, :].rearrange("p (c t d) -> p c t d", c=4, t=2)
        RB = rb[pb][:, :].rearrange("p (c t d) -> p c t d", c=4, t=2)
        QK = qk_tm[pb][:, :].rearrange("p (c t d) -> p c t d", c=4, t=2)
        do(DVE, lambda e: e.tensor_tensor(out=RA[:, cs], in0=T, in1=cosb, op=ALU.mult),
           reads=[bankR[pbank], tabR], writes=[rabR[pb]])
        do(DVE, lambda e: e.tensor_tensor(out=RB[:, cs], in0=T, in1=sinb, op=ALU.mult),
           reads=[bankR[pbank], tabR], writes=[rabR[pb]])
        do(POOL, lambda e: e.tensor_tensor(out=QK[:, cs, 0, :], in0=RA[:, cs, 0, :], in1=RB[:, cs, 1, :], op=ALU.subtract),
           reads=[rabR[pb]], writes=[qkR[pb]])
        do(POOL, lambda e: e.tensor_tensor(out=QK[:, cs, 1, :], in0=RB[:, cs, 0, :], in1=RA[:, cs, 1, :], op=ALU.add),
           reads=[rabR[pb]], writes=[qkR[pb]])
        def finish_block(att_j=None):
            fns = [lambda e: e.transpose(misc_bf[:, 0:128], qk_tm[pb][:, 128:256], ident_b[:, :])]
            rds = [qkR[pb], constR]
            if own:
                fns.append(lambda e: e.transpose(misc_bf[:, 128:256], qk_tm[pb][:, 0:128], ident_b[:, :]))
            if att_j is not None:
                fns.append(lambda e: e.transpose(misc_bf[:, 256:384], att16[:, att_j, :], ident_b[:, :]))
                rds.append(att16R[att_j])
            group(PE, fns, reads=rds, writes=[bankR[7]])
            do(DVE, lambda e: e.tensor_copy(out=KT[hb][:, tb * 128:(tb + 1) * 128], in_=misc_bf[:, 0:128]),
               reads=[bankR[7]], writes=[KTR[hb][tb]])
            if own:
                do(DVE, lambda e: e.tensor_copy(out=QT[hb][:, j * 128:(j + 1) * 128], in_=misc_bf[:, 128:256]),
                   reads=[bankR[7]], writes=[QTR[hb][j]])
            if att_j is not None:
                do(DVE, lambda e: e.tensor_copy(out=attT[:, att_j * 128:(att_j + 1) * 128], in_=misc_bf[:, 256:384]),
                   reads=[bankR[7]], writes=[attTR[att_j]])
        proj_pending.append(finish_block)

    class Batch:
        pass

    def make_batches(h):
        out = []
        for j in range(max_j):
            regs = list(range(2 * j))
            bl = []
            while regs:
                bl.append(("r", regs[:4]))
                regs = regs[4:]
            bl.append(("s", [2 * j, 2 * j + 1]))
            for i, (kind, slots) in enumerate(bl):
                b = Batch()
                b.h, b.j, b.kind, b.slots = h, j, kind, slots
                b.first = (i == 0)
                b.last = (i == len(bl) - 1)
                out.append(b)
        return out

    cnt = {"st": 0, "pt": 0, "ptd": 0, "n": 0}

    def qk(b):
        hb = b.h % 2
        b.buf = cnt["st"] % 2
        cnt["st"] += 1
        fns = []
        for c in range(2):
            for i, s_ in enumerate(b.slots):
                fns.append(lambda e, c=c, i=i, s_=s_: e.matmul(
                    P8[:, 2 * b.buf + c, i * 128:(i + 1) * 128],
                    KT[hb][64 * c:64 * c + 64, s_ * 128:(s_ + 1) * 128],
                    QT[hb][64 * c:64 * c + 64, b.j * 128:(b.j + 1) * 128], start=True, stop=True))
        group(PE, fns, reads=[KTR[hb][s_] for s_ in b.slots] + [QTR[hb][b.j]],
              writes=[bankR[2 * b.buf], bankR[2 * b.buf + 1]])

    def ex(b):
        bk = [bankR[2 * b.buf], bankR[2 * b.buf + 1]]
        n = len(b.slots)
        stv = P8[:, 2 * b.buf:2 * b.buf + 2, :]
        if b.kind == "r":
            b.pi = cnt["pt"] % 3
            cnt["pt"] += 1
            pt = PT[b.pi]
            b.pt, b.ptR = pt, PTR[b.pi]
            do(ACT, lambda e: e.activation(out=pt[:, :, 0:n * 128], in_=stv[:, :, 0:n * 128], func=AF.Exp,
                                           bias=cst[:, 0:1], scale=0.125), reads=bk + [constR], writes=[b.ptR])
        else:
            b.pi = cnt["ptd"] % 2
            cnt["ptd"] += 1
            pt = PTd[b.pi]
            b.pt, b.ptR = pt, PTdR[b.pi]
            do(ACT, lambda e: e.activation(out=pt[:, :, 0:128], in_=stv[:, :, 0:128], func=AF.Exp,
                                           bias=mb[:, b.j:b.j + 1], scale=0.125), reads=bk + [constR], writes=[b.ptR])
            do(ACT, lambda e: e.activation(out=pt[0:64, :, 128:256], in_=stv[0:64, :, 128:256], func=AF.Exp,
                                           bias=cst[0:64, 0:1], scale=0.125), reads=bk, writes=[b.ptR])
            do(ACT, lambda e: e.activation(out=pt[64:128, :, 192:256], in_=stv[64:128, :, 192:256], func=AF.Exp,
                                           bias=cst[64:128, 0:1], scale=0.125), reads=bk, writes=[b.ptR])

    def pvmm(b):
        hb = b.h % 2
        n = len(b.slots)
        if b.first:
            cnt["ob"] = cnt.get("ob", -1) + 1
        ob = cnt["ob"] % 2
        b.ob = ob
        fns = []
        for c in range(2):
            for i, s_ in enumerate(b.slots):
                fns.append(lambda e, c=c, i=i, s_=s_: e.matmul(
                    P8[:, 4 + ob, c * 129:(c + 1) * 129], b.pt[:, c, i * 128:(i + 1) * 128], VX[hb][:, s_, 0:129],
                    start=(b.first and c == 0 and i == 0), stop=False, skip_group_check=True))
        group(PE, fns, reads=[b.ptR] + [VXR[hb][s_] for s_ in b.slots], writes=[bankR[4 + ob]])

    def normalize(b):
        ob = cnt["n"] % 2
        cnt["n"] += 1
        j = b.j
        oc, st_, d_ = Oc[ob], nst[ob], dd[ob]
        obank = P8[:, 4 + b.ob, 0:258]
        ov = obank.rearrange("p (c v) -> p c v", c=2)
        oR = bankR[4 + b.ob]
        do(DVE, lambda e: e.reciprocal(out=st_[:, 0:2], in_=ov[:, :, 128]), reads=[oR], writes=[nstR[ob]])
        do(DVE, lambda e: e.tensor_tensor(out=st_[:, 2:3], in0=st_[:, 1:2], in1=lamc[:, 5:6], op=ALU.mult),
           reads=[nstR[ob], lamR], writes=[nstR[ob]])
        do(DVE, lambda e: e.tensor_scalar(out=d_[:, :], in0=ov[:, 0, 0:128], scalar1=st_[:, 0:1], scalar2=None,
                                          op0=ALU.mult), reads=[oR, nstR[ob]], writes=[ddR[ob]])
        do(DVE, lambda e: e.scalar_tensor_tensor(out=A16[:, j, :], in0=ov[:, 1, 0:128], scalar=st_[:, 2:3], in1=d_[:, :],
                                                 op0=ALU.mult, op1=ALU.add), reads=[oR, nstR[ob], ddR[ob]],
           writes=[A16R[j]])
        do(DVE, lambda e: e.scalar_tensor_tensor(out=junk[:, :], in0=A16[:, j, :], scalar=1.0, in1=A16[:, j, :],
                                                 op0=ALU.mult, op1=ALU.mult, accum_out=ss16[:, 0, j:j + 1]),
           reads=[A16R[j]], writes=[ss16R])

    def finalize_head():
        do(ACT, lambda e: e.activation(out=ss16[:, 1, :], in_=ss16[:, 0, :], func=AF.Sqrt, bias=cst[:, 2:3],
                                       scale=1.0 / 128.0), reads=[ss16R, constR], writes=[ss16R])
        do(DVE, lambda e: e.reciprocal(out=ss16[:, 2, :], in_=ss16[:, 1, :]), reads=[ss16R], writes=[ss16R])
        for j in range(NO):
            do(DVE, lambda e, j=j: e.scalar_tensor_tensor(out=att16[:, j, :], in0=A16[:, j, :], scalar=ss16[:, 2, j:j + 1],
                                                          in1=sg08[:, :], op0=ALU.mult, op1=ALU.mult),
               reads=[A16R[j], ss16R, constR], writes=[att16R[j]])

    def att_transpose(j):
        group(PE, [lambda e: e.transpose(misc_bf[:, 256:384], att16[:, j, :], ident_b[:, :])],
              reads=[att16R[j], constR], writes=[bankR[7]])
        do(DVE, lambda e: e.tensor_copy(out=attT[:, j * 128:(j + 1) * 128], in_=misc_bf[:, 256:384]),
           reads=[bankR[7]], writes=[attTR[j]])

    def gates_merge(h):
        for tg in range(4):
            bA = 2 * (tg % 2)
            bC = bA + 1
            sl = slice(tg * 512, (tg + 1) * 512)
            rhs = [xT_blk[k][:, 8 * tg + 1:8 * tg + 8:2, :] for k in range(KC)]
            for bnk, off in ((bA, 0), (bC, 128)):
                group(PE, [lambda e, k=k: e.matmul(P8[:, bnk, :].rearrange("p (b t) -> p b t", t=128),
                                                   wg[:, k, off:off + 128], rhs[k], start=(k == 0), stop=(k == KC - 1))
                           for k in range(KC)], reads=[xTR, wgR], writes=[bankR[bnk]])
            do(ACT, lambda e: e.activation(out=sA[:, :], in_=P8[:, bA, :], func=AF.Sigmoid, bias=pv[:, 16 + h:17 + h],
                                           scale=1.0), reads=[bankR[bA]], writes=[sAR])
            do(ACT, lambda e: e.activation(out=sC[:, :], in_=P8[:, bC, :], func=AF.Sigmoid, bias=pv[:, 24 + h:25 + h],
                                           scale=1.0), reads=[bankR[bC]], writes=[sCR])
            do(DVE, lambda e: e.tensor_tensor(out=sA[:, :], in0=sA[:, :], in1=attT[:, sl], op=ALU.mult),
               reads=attTR[4 * tg:4 * tg + 4], writes=[sAR])
            do(DVE, lambda e: e.tensor_tensor(out=sC[:, :], in0=sC[:, :], in1=cm[:, h, sl], op=ALU.mult),
               reads=[cmR[h][tg]], writes=[sCR])
            do(DVE, lambda e: e.tensor_tensor(out=cm[:, h, sl], in0=sA[:, :], in1=sC[:, :], op=ALU.add),
               reads=[sAR, sCR], writes=[cmR[h][tg]])

    NHEADS = H if stop_after not in ("a1", "aq") else 1
    load_wqkv(0)
    if NHEADS > 1:
        load_wqkv(1)
    load_wg(0)
    for tb in range(NB):
        proj_block(0, tb, pbank=(6 if tb % 2 == 0 else 0))
        flush_proj(3)
    flush_proj(0)
    for h in range(NHEADS):
        if h + 2 < NHEADS:
            load_wqkv(h + 2)
        batches = make_batches(h)
        first_pending = False
        def warm_burst():
            group(PE, [lambda e: e.matmul(P8[:, 6, 0:128], xT_bf[:, 0, 0:128], xT_bf[:, 0, 0:128],
                                          start=True, stop=True) for _ in range(36)],
                  reads=[xTR], writes=[bankR[6]])
        last_head = (h == NHEADS - 1 and NHEADS > 1)
        qk(batches[0])
        for i, b in enumerate(batches):
            ex(b)
            if i + 1 < len(batches):
                qk(batches[i + 1])
            pvmm(b)
            if b.first:
                first_pending = True
            if first_pending and (b.last or not b.first):
                first_pending = False
                if h + 1 < NHEADS:
                    flush_proj(2)
                    proj_block(h + 1, 2 * b.j)
            if b.last and last_head and b.j in (1, 4, 8, 12):
                warm_burst()
            if b.last:
                att_done = False
                if h + 1 < NHEADS:
                    while len(proj_pending) > 2:
                        fb = proj_pending.pop(0)
                        if h > 0 and not att_done:
                            fb(att_j=b.j)
                            att_done = True
                        else:
                            fb()
                if h + 1 < NHEADS:
                    proj_block(h + 1, 2 * b.j + 1)
                normalize(b)
                if h > 0 and not att_done:
                    att_transpose(b.j)
        flush_proj(0)
        finalize_head()
        if h > 0:
            gates_merge(h - 1)
            load_wg(h)
    for j in range(max_j):
        att_transpose(j)
    if stop_after == "aq":
        barrier()
        dq = Dq(nc, "dbg")
        t = dma(POOL, dq, dbg[:, 0:2048], attT[:, :])
        POOL.wait([t])
        return nc
    gates_merge(NHEADS - 1)
    barrier()
    esA.close()

    if stop_after in ("a", "a1"):
        d0 = sb("d0", [128, 16384], F32)
        for c in range(KC):
            do(DVE, lambda e: e.tensor_copy(out=d0[:, c * 2048:(c + 1) * 2048], in_=cm[:, c, :]), reads=cmR[c])
        return finish(d0[:, :], 16384)

    esX.close()
    esT = ExitStack()

    def sbT(name, shape, dtype):
        return esT.enter_context(nc.sbuf_tensor(name, shape, dtype))

    pr = sbT("pr", [128, 4096], F32)
    dq_pr = Dq(nc, "pr")
    t_pr = dma(SP, dq_pr, pr[:, :], prow[0:1, 128:4224].broadcast_to([128, 4096]))
    prR = Res()
    prR.w = [t_pr]
    Wout = sbT("Wout", [128, KC, D], BF16)
    WoutR = Res()
    dq_wout = Dq(nc, "wout")
    WoutRh = [Res(), Res()]
    dq_wout2 = Dq(nc, "wout2")
    for hf, dqw in ((0, dq_wout), (1, dq_wout2)):
        t_w = dma(POOL, dqw, Wout[:, :, hf * 512:(hf + 1) * 512],
                  w_out[:, hf * 512:(hf + 1) * 512].rearrange("(k p) n -> p k n", p=128))
        WoutRh[hf].w = [t_w]
    WoutR.w = WoutRh[0].w + WoutRh[1].w
    NW1, NW2 = 4, 4
    W1 = [sbT("W1_%d" % i, [128, KC, 256], BF16) for i in range(NW1)]
    W1R = [Res() for _ in range(NW1)]
    dq_w1 = [Dq(nc, "w1_%d" % i) for i in range(NW1)]
    W2 = [sbT("W2_%d" % i, [128, 4, 512], BF16) for i in range(NW2)]
    W2R = [Res() for _ in range(NW2)]
    dq_w2 = [Dq(nc, "w2_%d" % i) for i in range(NW2)]
    xo = [sbT("xo%d" % i, [128, D], F32) for i in range(2)]
    xoR = [Res() for _ in range(2)]
    dq_xo = [Dq(nc, "xo%d" % i) for i in range(2)]
    r2 = sbT("r2", [128, 4, D], F32)
    r2R = [Res() for _ in range(4)]
    dq_y = [Dq(nc, "y%d" % i) for i in range(4)]
    h1 = [sbT("h1_%d" % i, [128, 4, D], F32) for i in range(2)]
    h1R = [[Res() for _ in range(4)] for _ in range(2)]
    h1b = sbT("h1b", [128, 4, D], BF16)
    h1bR = [Res() for _ in range(4)]
    h1T = sbT("h1T", [128, KC, 512], BF16)
    h1TR = Res()
    hr = [sbT("hr%d" % i, [128, 512], F32) for i in range(2)]
    hrR = [Res(), Res()]
    hidT = sbT("hidT", [128, 32, 512], BF16)
    hidR = [Res() for _ in range(32)]
    bst = [sbT("bst%d" % i, [128, 2, 6], F32) for i in range(2)]
    bmv = [sbT("bmv%d" % i, [128, 4], F32) for i in range(2)]
    bstR = [Res(), Res()]
    lcount = [0]

    def layer_norm(src, dst, g_ap, b_ap, srcR, dstR):
        i = lcount[0] % 2
        lcount[0] += 1
        st, mv = bst[i], bmv[i]
        do(DVE, lambda e: e.bn_stats(out=st[:, 0, :], in_=src[:, 0:512]), reads=[srcR], writes=[bstR[i]])
        do(DVE, lambda e: e.bn_stats(out=st[:, 1, :], in_=src[:, 512:1024]), reads=[srcR], writes=[bstR[i]])
        do(DVE, lambda e: e.bn_aggr(out=mv[:, 0:2], in_=st[:, :, :].rearrange("p a b -> p (a b)")),
           reads=[bstR[i]], writes=[bstR[i]])
        do(ACT, lambda e: e.activation(out=mv[:, 2:3], in_=mv[:, 1:2], func=AF.Sqrt, bias=cst[:, 2:3], scale=1.0),
           reads=[bstR[i], constR], writes=[bstR[i]])
        do(DVE, lambda e: e.reciprocal(out=mv[:, 3:4], in_=mv[:, 2:3]), reads=[bstR[i]], writes=[bstR[i]])
        rr = [srcR, bstR[i]] if srcR is not dstR else [bstR[i]]
        do(DVE, lambda e: e.tensor_scalar(out=dst, in0=src, scalar1=mv[:, 0:1], scalar2=mv[:, 3:4],
                                          op0=ALU.subtract, op1=ALU.mult), reads=rr, writes=[dstR])
        do(DVE, lambda e: e.tensor_tensor(out=dst, in0=dst, in1=g_ap, op=ALU.mult), reads=[prR], writes=[dstR])
        do(DVE, lambda e: e.tensor_tensor(out=dst, in0=dst, in1=b_ap, op=ALU.add), reads=[prR], writes=[dstR])

    w1_issued = [0]
    w2_issued = [0]

    def issue_w1(n):
        while w1_issued[0] < n and w1_issued[0] < 4 * 16:
            q = w1_issued[0]
            g = q % 16
            i = q % NW1
            dma(POOL, dq_w1[i], W1[i][:, :, :], w_ff1[:, g * 256:(g + 1) * 256].rearrange("(k p) n -> p k n", p=128),
                writes=[W1R[i]])
            w1_issued[0] += 1

    def issue_w2(n):
        while w2_issued[0] < n and w2_issued[0] < 4 * 16:
            q = w2_issued[0]
            hf, g4 = (q % 16) // 8, q % 8
            i = q % NW2
            dma(POOL, dq_w2[i], W2[i][:, :, :],
                w_ff2[g4 * 512:(g4 + 1) * 512, hf * 512:(hf + 1) * 512].rearrange("(fl p) n -> p fl n", p=128),
                writes=[W2R[i]])
            w2_issued[0] += 1

    def t1_block_mm(tg, tbl):
        hp = tg % 2
        tb = 4 * tg + tbl
        xs = tbl % 2
        dma(SP, dq_xo[xs], xo[xs][:, :], xown[tb * 128:(tb + 1) * 128, :], writes=[xoR[xs]])
        for hf in range(2):
            hs = slice(hf * 512, (hf + 1) * 512)
            mbk = 4 + (2 * tbl + hf) % 4
            group(PE, [lambda e, c=c: e.matmul(P8[:, mbk, :], cm[:, c, tb * 128:(tb + 1) * 128], Wout[:, c, hs],
                                               start=(c == 0), stop=(c == KC - 1)) for c in range(KC)],
                  reads=[cmR[c][tg] for c in range(KC)] + [WoutRh[hf]], writes=[bankR[mbk]])
            do(DVE, lambda e: e.scalar_tensor_tensor(out=h1[hp][:, tbl, hs], in0=xo[xs][:, hs], scalar=ALPHA,
                                                     in1=P8[:, mbk, :], op0=ALU.mult, op1=ALU.add),
               reads=[xoR[xs], bankR[mbk]], writes=[h1R[hp][tbl]])

    def t1_block_ln(tg, tbl):
        hp = tg % 2
        layer_norm(h1[hp][:, tbl, :], h1[hp][:, tbl, :], pr[:, 0:1024], pr[:, 1024:2048], h1R[hp][tbl], h1R[hp][tbl])
        do(DVE, lambda e: e.tensor_copy(out=h1b[:, tbl, :], in_=h1[hp][:, tbl, :]),
           reads=[h1R[hp][tbl]], writes=[h1bR[tbl]])

    def t1_matmuls(tg):
        for tbl in range(4):
            t1_block_mm(tg, tbl)

    def t1_ln(tg):
        for tbl in range(4):
            t1_block_ln(tg, tbl)

    def t1_transposes(tg):
        for tbl in range(4):
            trb = 6 + (tbl % 2)
            trv = P8[:, trb, :].bitcast(BF16)
            group(PE, [lambda e, c=c: e.transpose(trv[:, c * 128:(c + 1) * 128], h1b[:, tbl, c * 128:(c + 1) * 128],
                                                  ident_b[:, :]) for c in range(KC)],
                  reads=[h1bR[tbl], constR], writes=[bankR[trb]])
            do(ACT, lambda e: e.copy(out=h1T[:, :, tbl * 128:(tbl + 1) * 128],
                                     in_=trv.rearrange("p (c t) -> p c t", t=128)),
               reads=[bankR[trb]], writes=[h1TR])

    def ff1(tg):
        for f in range(32):
            g, fl = f // 2, f % 2
            q = tg * 16 + g
            wi = q % NW1
            hb_ = 4 + (f % 2)
            group(PE, [lambda e, c=c: e.matmul(P8[:, hb_, :], W1[wi][:, c, fl * 128:(fl + 1) * 128], h1T[:, c, :],
                                               start=(c == 0), stop=(c == KC - 1)) for c in range(KC)],
                  reads=[W1R[wi], h1TR], writes=[bankR[hb_]])
            if fl == 1:
                issue_w1(q + 1 + NW1)
            do(ACT, lambda e: e.activation(out=hr[f % 2][:, :], in_=P8[:, hb_, :], func=AF.Relu, bias=cst[:, 0:1],
                                           scale=1.0), reads=[bankR[hb_], constR], writes=[hrR[f % 2]])
            do(DVE, lambda e: e.tensor_tensor(out=hidT[:, f, :], in0=hr[f % 2][:, :], in1=hr[f % 2][:, :], op=ALU.mult),
               reads=[hrR[f % 2]], writes=[hidR[f]])

    def ff2(tg, hf):
        hp = tg % 2
        hs = slice(hf * 512, (hf + 1) * 512)
        for g4 in range(8):
            q = tg * 16 + hf * 8 + g4
            wi = q % NW2
            fns = []
            for tbl in range(4):
                for fl in range(4):
                    f = 4 * g4 + fl
                    fns.append(lambda e, tbl=tbl, fl=fl, f=f: e.matmul(
                        P8[:, tbl, :], hidT[:, f, tbl * 128:(tbl + 1) * 128], W2[wi][:, fl, :],
                        start=(g4 == 0 and fl == 0), stop=(g4 == 7 and fl == 3)))
            group(PE, fns, reads=[W2R[wi]] + [hidR[4 * g4 + fl] for fl in range(4)],
                  writes=[bankR[0], bankR[1], bankR[2], bankR[3]])
            issue_w2(q + 1 + NW2)
        for tbl in range(4):
            do(DVE, lambda e: e.scalar_tensor_tensor(out=r2[:, tbl, hs], in0=h1[hp][:, tbl, hs], scalar=ALPHA,
                                                     in1=P8[:, tbl, :], op0=ALU.mult, op1=ALU.add),
               reads=[h1R[hp][tbl], bankR[tbl]], writes=[r2R[tbl]])

    def ln2_store(tg):
        for tbl in range(4):
            tb = 4 * tg + tbl
            layer_norm(r2[:, tbl, :], r2[:, tbl, :], pr[:, 2048:3072], pr[:, 3072:4096], r2R[tbl], r2R[tbl])
            dma(SP, dq_y[tbl], y[tb * 128:(tb + 1) * 128, :], r2[:, tbl, :], reads=[r2R[tbl]])

    issue_w1(NW1)
    issue_w2(NW2)
    for tbl in range(4):
        t1_block_mm(0, tbl)
        if tbl >= 1:
            t1_block_ln(0, tbl - 1)
    t1_block_ln(0, 3)
    t1_transposes(0)
    for tg in range(4):
        ff1(tg)
        if tg > 0:
            ln2_store(tg - 1)
        ff2(tg, 0)
        if tg + 1 < 4:
            t1_matmuls(tg + 1)
            t1_ln(tg + 1)
        ff2(tg, 1)
        if tg + 1 < 4:
            t1_transposes(tg + 1)
    ln2_store(3)
    SP.wait([(q_, q_.cnt) for q_ in dq_y])
    barrier()
    esT.close()
    return nc


def ctx_order(half):
    if half == 1:
        return list(range(NB))
    o = []
    for i in range(0, NB, 2):
        o += [i + 1, i]
    return o


def make_in_maps(inp):
    x = np.asarray(inp["x"], dtype=np.float32)
    positions = np.asarray(inp["positions"]).astype(np.int32)
    f = lambda k: np.ascontiguousarray(np.asarray(inp[k], dtype=np.float32)[0])
    w_in, w_pw2, w_out, w_ff1, w_ff2 = f("w_in"), f("w_pw2"), f("w_out"), f("w_ff1"), f("w_ff2")
    col = lambda v, n: np.asarray(v, dtype=np.float32).reshape(n, 128).T
    dwk = np.asarray(inp["dw_kernel"], dtype=np.float32)[0]
    dwk_cols = dwk.reshape(31, 8, 128).transpose(2, 1, 0).reshape(128, 248)
    pvec = np.concatenate([
        col(inp["b_glu"][0], 16), col(inp["b_gate"][0], 16), col(inp["dw_bias"][0], 8),
        col(inp["conv_ln_g"][0], 8), col(inp["conv_ln_b"][0], 8), col(inp["b_pw2"][0], 8),
        dwk_cols], axis=1).astype(np.float32)
    assert pvec.shape == (128, NPV)
    prow = np.concatenate([
        np.asarray(inp[k], dtype=np.float32).reshape(-1) for k in
        ["subln_g", "ln1_g", "ln1_b", "ln2_g", "ln2_b", "lambda_q1", "lambda_k1", "lambda_q2", "lambda_k2"]
    ]).reshape(1, NPR).astype(np.float32)
    ident = np.eye(128, dtype=np.float32)
    maps = []
    for core in range(8):
        b, half = core // 2, core % 2
        order = ctx_order(half)
        tok_idx = np.concatenate([np.arange(g * 128, (g + 1) * 128) for g in order])
        xb = x[b]
        xT = np.ascontiguousarray(xb[tok_idx].T)
        own_blocks = [2 * j + half for j in range(NO)]
        own_idx = np.concatenate([np.arange(g * 128, (g + 1) * 128) for g in own_blocks])
        xown = np.ascontiguousarray(xb[own_idx])
        xh = np.zeros((NO * 32, D), dtype=np.float32)
        for j, g in enumerate(own_blocks):
            if g > 0:
                xh[j * 32:(j + 1) * 32] = xb[g * 128 - 32:g * 128]
        xhT = np.ascontiguousarray(xh.T)
        pos_ctx = positions[b][tok_idx]
        pos_tm = np.ascontiguousarray(pos_ctx.reshape(NB, 128).T).astype(np.int32)
        pos_blk = np.ascontiguousarray(pos_ctx.reshape(NB, 128)[:, 0].reshape(1, NB)).astype(np.int32)
        maps.append(dict(xT=xT, xhT=xhT, xown=xown, pos_tm=pos_tm, pos_blk=pos_blk, w_in=w_in, w_pw2=w_pw2,
                         w_out=w_out, w_ff1=w_ff1, w_ff2=w_ff2, pvec=pvec, prow=prow, ident=ident))
    return maps


def kernel(**inputs):
    maps = make_in_maps(inputs)
    nc = build_program()
    res = run_bass_kernel_spmd(nc, maps, core_ids=list(range(8)))
    out = np.zeros((4, SEQ, D), dtype=np.float32)
    for core in range(8):
        b, half = core // 2, core % 2
        yc = np.asarray(res.results[core]["y"], dtype=np.float32)
        for j in range(NO):
            g = 2 * j + half
            out[b, g * 128:(g + 1) * 128] = yc[j * 128:(j + 1) * 128]
    return out
```

```python
import math
import numpy as np
from contextlib import ExitStack
import concourse.bass as bass
import concourse.mybir as mybir
from concourse.bass_utils import run_bass_kernel_spmd

F32 = mybir.dt.float32
BF16 = mybir.dt.bfloat16
I32 = mybir.dt.int32
AF = mybir.ActivationFunctionType
ALU = mybir.AluOpType
AX = mybir.AxisListType

D = 1024
KC = 8
SEQ = 4096
NB = 32
NO = 16
H = 8
TOK = 2048
LN_EPS = 1e-5
ALPHA = 2.0 ** 0.25
LAM_INIT = 0.8 - 0.6 * math.exp(0.0)
NPV = 312
NPR = 4480
PI = math.pi


class Eng:
    def __init__(self, nc, eng, name, is_pe=False):
        self.e = eng
        self.name = name
        self.sem = nc.alloc_semaphore("s_" + name)
        self.cnt = 0
        self.seen = {}
        self.is_pe = is_pe

    def wait(self, deps):
        for src, val in deps:
            if src is self and self.is_pe:
                continue
            if self.seen.get(src, 0) >= val:
                continue
            self.e.wait_ge(src.sem, val)
            self.seen[src] = val

    def sig(self, ins):
        ins.then_inc(self.sem, 1)
        self.cnt += 1
        return (self, self.cnt)


class Dq:
    registry = []

    def __init__(self, nc, name):
        self.sem = nc.alloc_semaphore("d_" + name)
        self.cnt = 0
        Dq.registry.append(self)


class Res:
    def __init__(self):
        self.w = []
        self.r = {}


def _deps(reads, writes, extra):
    deps = list(extra)
    for b in reads:
        deps += b.w
    for b in writes:
        deps += b.w
        deps += list(b.r.items())
    return deps


def _upd(t, reads, writes):
    for b in reads:
        b.r[t[0]] = t[1]
    for b in writes:
        b.w = [t]
        b.r = {}


def do(eng, fn, reads=(), writes=(), extra=()):
    eng.wait(_deps(reads, writes, extra))
    t = eng.sig(fn(eng.e))
    _upd(t, reads, writes)
    return t


def group(eng, fns, reads=(), writes=(), extra=()):
    eng.wait(_deps(reads, writes, extra))
    ins = None
    for fn in fns:
        ins = fn(eng.e)
    t = eng.sig(ins)
    _upd(t, reads, writes)
    return t


def dma(queue, dq, out, in_, reads=(), writes=(), extra=()):
    queue.wait(_deps(reads, writes, extra))
    ins = queue.e.dma_start(out=out, in_=in_)
    ins.then_inc(dq.sem, 16)
    dq.cnt += 16
    t = (dq, dq.cnt)
    _upd(t, reads, writes)
    return t


def build_program(stop_after=None, skip_b=False, max_j=NO):
    nc = bass.Bass("TRN2", target_bir_lowering=False)
    Dq.registry = []
    dt = nc.dram_tensor
    xT = dt("xT", [D, SEQ], F32, kind="ExternalInput").ap()
    xhT = dt("xhT", [D, 512], F32, kind="ExternalInput").ap()
    xown = dt("xown", [TOK, D], F32, kind="ExternalInput").ap()
    pos_tm = dt("pos_tm", [128, NB], I32, kind="ExternalInput").ap()
    pos_blk = dt("pos_blk", [1, NB], I32, kind="ExternalInput").ap()
    w_in = dt("w_in", [D, 7168], F32, kind="ExternalInput").ap()
    w_pw2 = dt("w_pw2", [D, D], F32, kind="ExternalInput").ap()
    w_out = dt("w_out", [D, D], F32, kind="ExternalInput").ap()
    w_ff1 = dt("w_ff1", [D, 4096], F32, kind="ExternalInput").ap()
    w_ff2 = dt("w_ff2", [4096, D], F32, kind="ExternalInput").ap()
    pvec = dt("pvec", [128, NPV], F32, kind="ExternalInput").ap()
    prow = dt("prow", [1, NPR], F32, kind="ExternalInput").ap()
    ident_in = dt("ident", [128, 128], F32, kind="ExternalInput").ap()
    y = dt("y", [TOK, D], F32, kind="ExternalOutput").ap()
    dbg = None
    if stop_after is not None:
        dbg = dt("dbg", [128, 16384], F32, kind="ExternalOutput").ap()

    es = ExitStack()

    def sb(name, shape, dtype):
        return es.enter_context(nc.sbuf_tensor(name, shape, dtype))

    PE = Eng(nc, nc.tensor, "pe", is_pe=True)
    ACT = Eng(nc, nc.scalar, "act")
    DVE = Eng(nc, nc.vector, "dve")
    POOL = Eng(nc, nc.gpsimd, "pool")
    SP = Eng(nc, nc.sync, "sp")

    P8 = es.enter_context(nc.psum_tensor("P8", [128, 8, 512], F32))
    bankR = [Res() for _ in range(8)]

    cm = sb("cm", [128, KC, TOK], BF16)
    pv = sb("pv", [128, NPV], F32)
    ident_f = sb("ident_f", [128, 128], F32)
    ident_b = sb("ident_b", [128, 128], BF16)
    ones_b = sb("ones_b", [128, 128], BF16)
    cst = sb("cst", [128, 8], F32)
    posi = sb("posi", [128, NB], I32)
    posbi = sb("posbi", [128, NB], I32)
    posf = sb("posf", [128, NB], F32)
    posbf = sb("posbf", [128, NB], F32)
    invf = sb("invf", [128, 32], F32)
    mb = sb("mb", [128, NO], F32)
    hv = sb("hv", [128, NO], F32)
    lamc = sb("lamc", [128, 8], F32)
    sg08 = sb("sg08", [128, 128], F32)
    esX = ExitStack()
    xT_bf = esX.enter_context(nc.sbuf_tensor("xT_bf", [128, KC, SEQ], BF16))
    cosT = esX.enter_context(nc.sbuf_tensor("cosT", [128, NB, 32], F32))
    sinT = esX.enter_context(nc.sbuf_tensor("sinT", [128, NB, 32], F32))

    xTR = Res()
    cmR = [[Res() for _ in range(4)] for _ in range(KC)]
    constR = Res()

    esB = ExitStack()
    if skip_b:
        for c_ in range(KC):
            do(POOL, lambda e, c_=c_: e.memset(cm[:, c_, :], 0.0), writes=cmR[c_])

    def sbB(name, shape, dtype):
        return esB.enter_context(nc.sbuf_tensor(name, shape, dtype))

    xh_bf = sbB("xh_bf", [128, KC, 512], BF16)
    xhR = Res()
    dq_xh = Dq(nc, "xh")
    dma(POOL, dq_xh, xh_bf[:, :, :], xhT.rearrange("(k p) n -> p k n", p=128), writes=[xhR])

    esB1 = ExitStack()
    wgl = esB1.enter_context(nc.sbuf_tensor("wgl", [128, KC, 2048], BF16))
    wglR = [Res() for _ in range(4)]
    dq_wgl = [Dq(nc, "wgl%d" % i) for i in range(4)]
    def load_wgl(g4):
        dma(POOL, dq_wgl[g4], wgl[:, :, g4 * 512:(g4 + 1) * 512],
            w_in[:, 3072 + g4 * 512:3072 + (g4 + 1) * 512].rearrange("(k p) n -> p k n", p=128), writes=[wglR[g4]])

    load_wgl(0)
    load_wgl(2)
    diag = [esB1.enter_context(nc.sbuf_tensor("diag%d" % i, [128, 31, 128], BF16)) for i in range(2)]
    diagR = [Res(), Res()]
    ubuf = [esB1.enter_context(nc.sbuf_tensor("ubuf%d" % i, [128, NO, 160], BF16)) for i in range(2)]
    ubufR = [Res(), Res()]
    sgb = [esB1.enter_context(nc.sbuf_tensor("sgb%d" % i, [128, 512], F32)) for i in range(2)]
    sgbR = [Res(), Res()]
    xT_blk = [xT_bf[:, k, :].rearrange("p (b t) -> p b t", t=128) for k in range(KC)]

    esP = ExitStack()
    pr0 = esP.enter_context(nc.sbuf_tensor("pr0", [128, 384], F32))
    dq_init = Dq(nc, "init")
    dq_x = Dq(nc, "x")

    t_pv = dma(SP, dq_init, pv[:, :], pvec[:, :])
    dma(SP, dq_init, pr0[:, 0:128], prow[0:1, 0:128].broadcast_to([128, 128]))
    dma(SP, dq_init, pr0[:, 128:384], prow[0:1, 4224:4480].broadcast_to([128, 256]))
    dma(SP, dq_init, posi[:, :], pos_tm[:, :])
    dma(SP, dq_init, posbi[:, :], pos_blk[0:1, :].broadcast_to([128, NB]))
    t_init = dma(SP, dq_init, ident_f[:, :], ident_in[:, :])
    xTRh = [Res(), Res()]
    dq_xb = Dq(nc, "xb")
    for hf, dqx in ((0, dq_x), (1, dq_xb)):
        for k in range(KC):
            t_x = dma(POOL, dqx, xT_bf[:, k, hf * 2048:(hf + 1) * 2048],
                      xT[k * 128:(k + 1) * 128, hf * 2048:(hf + 1) * 2048])
        xTRh[hf].w = [t_x]
    xTR.w = xTRh[0].w + xTRh[1].w
    load_wgl(1)
    load_wgl(3)

    do(POOL, lambda e: e.memset(cst[:, 0:1], 0.0))
    do(POOL, lambda e: e.memset(cst[:, 1:2], -PI))
    do(POOL, lambda e: e.memset(cst[:, 2:3], LN_EPS))
    do(POOL, lambda e: e.memset(ones_b[:, :], 1.0 / 1024.0))
    for i in range(32):
        val = float(np.float32(10000.0) ** np.float32(-(2.0 * i) / 64.0))
        t_c = do(POOL, lambda e, i=i, val=val: e.memset(invf[:, i:i + 1], val))
    constR.w = [t_c]

    ini = [t_init]
    do(DVE, lambda e: e.tensor_copy(out=ident_b[:, :], in_=ident_f[:, :]), extra=ini, writes=[constR])
    do(DVE, lambda e: e.tensor_copy(out=posf[:, :], in_=posi[:, :]), writes=[constR])
    do(DVE, lambda e: e.tensor_copy(out=posbf[:, :], in_=posbi[:, :]), writes=[constR])
    es0 = ExitStack()

    def sb0(name, shape, dtype):
        return es0.enter_context(nc.sbuf_tensor(name, shape, dtype))

    ang = sb0("ang", [128, NB, 32], F32)
    ang2 = sb0("ang2", [128, NB, 32], F32)
    angn = sb0("angn", [128, NB, 32], F32)
    angi = sb0("angi", [128, NB, 32], I32)
    angR = Res()
    ang2R = Res()
    tabR = Res()
    C1 = 6.28125
    C2 = 2 * PI - C1
    do(DVE, lambda e: e.tensor_tensor(out=ang[:, :, :], in0=posf[:, :].unsqueeze(2).broadcast_to([128, NB, 32]),
                                      in1=invf[:, :].unsqueeze(1).to_broadcast([128, NB, 32]), op=ALU.mult),
       reads=[constR], writes=[angR])

    def range_reduced_sin(dst):
        do(DVE, lambda e: e.tensor_scalar(out=angi[:, :, :], in0=ang[:, :, :], scalar1=1.0 / (2 * PI), scalar2=None,
                                          op0=ALU.mult), reads=[angR], writes=[ang2R])
        do(DVE, lambda e: e.tensor_copy(out=angn[:, :, :], in_=angi[:, :, :]), reads=[ang2R], writes=[ang2R])
        do(DVE, lambda e: e.scalar_tensor_tensor(out=ang2[:, :, :], in0=angn[:, :, :], scalar=-C1, in1=ang[:, :, :],
                                                 op0=ALU.mult, op1=ALU.add), reads=[ang2R, angR], writes=[ang2R])
        do(DVE, lambda e: e.scalar_tensor_tensor(out=ang2[:, :, :], in0=angn[:, :, :], scalar=-C2, in1=ang2[:, :, :],
                                                 op0=ALU.mult, op1=ALU.add), reads=[ang2R], writes=[ang2R])
        do(DVE, lambda e: e.tensor_scalar(out=ang2[:, :, :], in0=ang2[:, :, :], scalar1=-3.141592, scalar2=3.141592,
                                          op0=ALU.max, op1=ALU.min), reads=[ang2R], writes=[ang2R])
        do(ACT, lambda e: e.activation(out=dst, in_=ang2[:, :, :], func=AF.Sin, bias=cst[:, 0:1], scale=1.0),
           reads=[ang2R, constR], writes=[tabR])

    range_reduced_sin(sinT[:, :, :])
    do(DVE, lambda e: e.tensor_scalar(out=ang[:, :, :], in0=ang[:, :, :], scalar1=0.5 * PI, scalar2=None,
                                      op0=ALU.add), reads=[angR, ang2R], writes=[angR])
    range_reduced_sin(cosT[:, :, :])

    lq = sb0("lq", [128, 128], F32)
    lqR = Res()
    do(DVE, lambda e: e.tensor_tensor(out=lq[:, 0:64], in0=pr0[:, 128:192], in1=pr0[:, 192:256], op=ALU.mult),
       writes=[lqR])
    do(DVE, lambda e: e.tensor_tensor(out=lq[:, 64:128], in0=pr0[:, 256:320], in1=pr0[:, 320:384], op=ALU.mult),
       writes=[lqR])
    lamR = Res()
    do(DVE, lambda e: e.reduce_sum(out=lamc[:, 0:1], in_=lq[:, 0:64], axis=AX.X), reads=[lqR], writes=[lamR])
    do(DVE, lambda e: e.reduce_sum(out=lamc[:, 1:2], in_=lq[:, 64:128], axis=AX.X), reads=[lqR], writes=[lamR])
    do(ACT, lambda e: e.activation(out=lamc[:, 2:4], in_=lamc[:, 0:2], func=AF.Exp, bias=cst[:, 0:1], scale=1.0),
       reads=[lamR, constR], writes=[lamR])
    do(DVE, lambda e: e.tensor_tensor(out=lamc[:, 4:5], in0=lamc[:, 2:3], in1=lamc[:, 3:4], op=ALU.subtract),
       reads=[lamR], writes=[lamR])
    do(DVE, lambda e: e.tensor_scalar(out=lamc[:, 5:6], in0=lamc[:, 4:5], scalar1=-1.0, scalar2=-LAM_INIT,
                                      op0=ALU.mult, op1=ALU.add), reads=[lamR], writes=[lamR])
    do(DVE, lambda e: e.tensor_scalar(out=sg08[:, :], in0=pr0[:, 0:128], scalar1=1.0 - LAM_INIT, scalar2=None,
                                      op0=ALU.mult), writes=[constR])
    pb2 = posbf[:, :].rearrange("p (j t) -> p j t", t=2)
    do(DVE, lambda e: e.tensor_tensor(out=mb[:, :], in0=pb2[:, :, 0], in1=pb2[:, :, 1], op=ALU.is_gt),
       reads=[constR], writes=[constR])
    do(DVE, lambda e: e.tensor_scalar(out=mb[:, :], in0=mb[:, :], scalar1=-30000.0, scalar2=None, op0=ALU.mult),
       reads=[constR], writes=[constR])
    do(DVE, lambda e: e.tensor_reduce(out=lamc[:, 6:7], in_=posbf[:, :], axis=AX.X, op=ALU.min),
       reads=[constR], writes=[lamR])
    do(DVE, lambda e: e.tensor_scalar(out=hv[:, :], in0=pb2[:, :, 1], scalar1=lamc[:, 6:7], scalar2=None,
                                      op0=ALU.is_gt), reads=[lamR, constR], writes=[constR])

    all_dq = Dq.registry

    def barrier(exclude=()):
        engs = [PE, ACT, DVE, POOL, SP]
        tk = [(e_, e_.cnt) for e_ in engs if e_.cnt > 0] + \
             [(q, q.cnt) for q in all_dq if q.cnt > 0 and q not in exclude]
        for e_ in engs:
            e_.wait([t_ for t_ in tk if t_[0] is not e_])

    es0.close()
    esP.close()

    def finish(src_ap, ncols):
        dq = Dq(nc, "dbg")
        for c0 in range(0, ncols, 2048):
            c1 = min(ncols, c0 + 2048)
            t = dma(SP, dq, dbg[:, c0:c1], src_ap[:, c0:c1],
                    extra=[(DVE, DVE.cnt), (ACT, ACT.cnt), (POOL, POOL.cnt), (PE, PE.cnt)])
        SP.wait([t])
        return nc

    if stop_after == "p0":
        d0 = sb("d0", [128, 4096], F32)
        do(DVE, lambda e: e.tensor_copy(out=d0[:, 0:1024], in_=cosT[:, :, :].rearrange("p a b -> p (a b)")), reads=[tabR])
        do(DVE, lambda e: e.tensor_copy(out=d0[:, 1024:2048], in_=sinT[:, :, :].rearrange("p a b -> p (a b)")), reads=[tabR])
        do(DVE, lambda e: e.tensor_copy(out=d0[:, 2048:2064], in_=mb[:, :]), reads=[constR])
        do(DVE, lambda e: e.tensor_copy(out=d0[:, 2064:2080], in_=hv[:, :]), reads=[constR])
        do(DVE, lambda e: e.tensor_copy(out=d0[:, 2080:2088], in_=lamc[:, :]), reads=[lamR])
        do(DVE, lambda e: e.tensor_copy(out=d0[:, 2088:2088 + 128], in_=xT_bf[:, 3, 1000:1128]), reads=[xTR])
        return finish(d0[:, 0:4096], 4096)

    gcount = 0
    ycount = 0
    for c in ([] if skip_b else range(KC)):
        dg = diag[c % 2]
        dgR = diagR[c % 2]
        for k in range(31):
            do(DVE, lambda e, k=k, c=c, dg=dg: e.tensor_scalar(
                out=dg[:, k, :], in0=ident_f[:, :], scalar1=pv[:, 64 + c * 31 + k:64 + c * 31 + k + 1], scalar2=None,
                op0=ALU.mult), writes=[dgR], extra=[t_pv, t_init])
        ub = ubuf[c % 2]
        ubR = ubufR[c % 2]
        for grp in range(5):
            pa = gcount % 2
            gcount += 1
            ga_ps = P8[:, pa, :]
            gb_ps = P8[:, 2 + pa, :]
            if grp == 0:
                rhs = [xh_bf[:, k, :] for k in range(KC)]
                rr = [xhR]
            else:
                tg = grp - 1
                rhs = [xT_blk[k][:, 8 * tg + 1:8 * tg + 8:2, :] for k in range(KC)]
                rr = [xTRh[tg // 2]]
            wr = wglR[c // 4]
            wrb = wglR[2 + c // 4]
            fa = [lambda e, k=k: e.matmul(ga_ps if grp == 0 else ga_ps.rearrange("p (b t) -> p b t", t=128),
                                          wgl[:, k, c * 128:(c + 1) * 128], rhs[k], start=(k == 0), stop=(k == KC - 1))
                  for k in range(KC)]
            group(PE, fa, reads=rr + [wr], writes=[bankR[pa]])
            fb = [lambda e, k=k: e.matmul(gb_ps if grp == 0 else gb_ps.rearrange("p (b t) -> p b t", t=128),
                                          wgl[:, k, 1024 + c * 128:1024 + (c + 1) * 128], rhs[k],
                                          start=(k == 0), stop=(k == KC - 1)) for k in range(KC)]
            group(PE, fb, reads=rr + [wrb], writes=[bankR[2 + pa]])
            sg = sgb[pa]
            do(ACT, lambda e: e.activation(out=sg[:, :], in_=gb_ps, func=AF.Sigmoid, bias=pv[:, 8 + c:9 + c], scale=1.0),
               reads=[bankR[2 + pa]], writes=[sgbR[pa]], extra=[t_pv])
            if grp == 0:
                o_ap = ub[:, :, 0:32]
                i0 = ga_ps.rearrange("p (b t) -> p b t", t=32)
                i1 = sg[:, :].rearrange("p (b t) -> p b t", t=32)
            else:
                o_ap = ub[:, 4 * tg:4 * tg + 4, 32:160]
                i0 = ga_ps.rearrange("p (b t) -> p b t", t=128)
                i1 = sg[:, :].rearrange("p (b t) -> p b t", t=128)
            do(DVE, lambda e: e.scalar_tensor_tensor(out=o_ap, in0=i0, scalar=pv[:, c:c + 1], in1=i1,
                                                     op0=ALU.add, op1=ALU.mult),
               reads=[bankR[pa], sgbR[pa]], writes=[ubR], extra=[t_pv])
            if grp == 0:
                do(DVE, lambda e: e.tensor_tensor(out=ub[:, :, 0:32], in0=ub[:, :, 0:32],
                                                  in1=hv[:, :].unsqueeze(2).broadcast_to([128, NO, 32]), op=ALU.mult),
                   reads=[constR], writes=[ubR])
        for tg in range(4):
            yb = 4 + (ycount % 2)
            ycount += 1
            y_ps = P8[:, yb, :]
            fc = [lambda e, k=k: e.matmul(y_ps.rearrange("p (b t) -> p b t", t=128), dg[:, k, :],
                                          ub[:, 4 * tg:4 * tg + 4, 2 + k:2 + k + 128], start=(k == 0), stop=(k == 30))
                  for k in range(31)]
            group(PE, fc, reads=[dgR, ubR], writes=[bankR[yb]])
            do(ACT, lambda e: e.activation(out=cm[:, c, tg * 512:(tg + 1) * 512], in_=y_ps, func=AF.Identity,
                                           bias=pv[:, 32 + c:33 + c], scale=1.0),
               reads=[bankR[yb]], writes=[cmR[c][tg]], extra=[t_pv])
    barrier()
    esB1.close()

    if stop_after == "b1":
        d0 = sb("d0", [128, 16384], F32)
        for c in range(KC):
            do(DVE, lambda e: e.tensor_copy(out=d0[:, c * 2048:(c + 1) * 2048], in_=cm[:, c, :]), reads=cmR[c])
        return finish(d0[:, :], 16384)

    esB2 = ExitStack()

    def sbB2(name, shape, dtype):
        return esB2.enter_context(nc.sbuf_tensor(name, shape, dtype))

    wpw = sbB2("wpw", [128, KC, D], BF16)
    wpwR = Res()
    dq_wpw = Dq(nc, "wpw")
    for hf in range(2):
        t_w = dma(POOL, dq_wpw, wpw[:, :, hf * 512:(hf + 1) * 512],
                  w_pw2[:, hf * 512:(hf + 1) * 512].rearrange("(k p) n -> p k n", p=128))
    wpwR.w = [t_w]
    ysq = [sbB2("ysq%d" % i, [128, 512], BF16) for i in range(2)]
    ysqR = [Res(), Res()]
    mean = [sbB2("mean%d" % i, [128, 512], F32) for i in range(2)]
    msq = [sbB2("msq%d" % i, [128, 512], F32) for i in range(2)]
    rstd = [sbB2("rstd%d" % i, [128, 512], F32) for i in range(2)]
    statR = [Res(), Res()]
    t1 = [sbB2("t1_%d" % i, [128, 512], F32) for i in range(2)]
    t1R = [Res(), Res()]
    zT = [sbB2("zT%d" % i, [128, KC, 512], BF16) for i in range(2)]
    zTR = [Res(), Res()]
    ocount = [0]
    S1 = P8[:, 6, :]
    S2 = P8[:, 7, :]

    def b2_stats(tg):
        sl = slice(tg * 512, (tg + 1) * 512)
        PE.wait(_deps([], [bankR[6], bankR[7]], []))
        for c in range(KC):
            q = ysq[c % 2]
            do(POOL, lambda e: e.tensor_tensor(out=q[:, :], in0=cm[:, c, sl], in1=cm[:, c, sl], op=ALU.mult),
               reads=[cmR[c][tg]], writes=[ysqR[c % 2]])
            group(PE, [lambda e: e.matmul(S1, ones_b[:, :], cm[:, c, sl], start=(c == 0), stop=(c == KC - 1))],
                  reads=[cmR[c][tg], constR])
            t_s = group(PE, [lambda e: e.matmul(S2, ones_b[:, :], q[:, :], start=(c == 0), stop=(c == KC - 1))],
                        reads=[ysqR[c % 2]])
        for bk in (6, 7):
            bankR[bk].w = [t_s]
            bankR[bk].r = {}

    def b2_norm_a(tg):
        i = tg % 2
        do(DVE, lambda e: e.tensor_copy(out=mean[i][:, :], in_=S1), reads=[bankR[6]], writes=[statR[i]])
        do(DVE, lambda e: e.tensor_tensor(out=msq[i][:, :], in0=S1, in1=mean[i][:, :], op=ALU.mult),
           reads=[bankR[6], statR[i]], writes=[statR[i]])
        do(DVE, lambda e: e.tensor_tensor(out=msq[i][:, :], in0=S2, in1=msq[i][:, :], op=ALU.subtract),
           reads=[bankR[7], statR[i]], writes=[statR[i]])

    def b2_norm_b(tg):
        i = tg % 2
        sl = slice(tg * 512, (tg + 1) * 512)
        do(ACT, lambda e: e.activation(out=msq[i][:, :], in_=msq[i][:, :], func=AF.Sqrt, bias=cst[:, 2:3], scale=1.0),
           reads=[statR[i], constR], writes=[statR[i]])
        do(DVE, lambda e: e.reciprocal(out=rstd[i][:, :], in_=msq[i][:, :]), reads=[statR[i]], writes=[statR[i]])
        z = zT[tg % 2]
        zR = zTR[tg % 2]
        for c in range(KC):
            tt = t1[c % 2]
            do(DVE, lambda e: e.tensor_tensor(out=tt[:, :], in0=cm[:, c, sl], in1=mean[i][:, :], op=ALU.subtract),
               reads=[cmR[c][tg], statR[i]], writes=[t1R[c % 2]])
            do(DVE, lambda e: e.tensor_tensor(out=tt[:, :], in0=tt[:, :], in1=rstd[i][:, :], op=ALU.mult),
               reads=[statR[i]], writes=[t1R[c % 2]])
            do(ACT, lambda e: e.activation(out=z[:, c, :], in_=tt[:, :], func=AF.Silu, bias=pv[:, 48 + c:49 + c],
                                           scale=pv[:, 40 + c:41 + c]), reads=[t1R[c % 2]], writes=[zR])

    def b2_pw2(tg):
        sl = slice(tg * 512, (tg + 1) * 512)
        z = zT[tg % 2]
        zR = zTR[tg % 2]
        for dc in range(KC):
            ob = ocount[0] % 2
            ocount[0] += 1
            o_ps = P8[:, ob, :]
            fo = [lambda e, c=c: e.matmul(o_ps, wpw[:, c, dc * 128:(dc + 1) * 128], z[:, c, :],
                                          start=(c == 0), stop=(c == KC - 1)) for c in range(KC)]
            group(PE, fo, reads=[wpwR, zR], writes=[bankR[ob]])
            do(ACT, lambda e: e.activation(out=cm[:, dc, sl], in_=o_ps, func=AF.Identity, bias=pv[:, 56 + dc:57 + dc],
                                           scale=1.0), reads=[bankR[ob]], writes=[cmR[dc][tg]])

    if not skip_b:
        b2_stats(0)
        b2_norm_a(0)
        b2_stats(1)
        b2_norm_b(0)
        b2_norm_a(1)
        for tg in range(4):
            if tg + 2 < 4:
                b2_stats(tg + 2)
            if tg + 1 < 4:
                b2_norm_b(tg + 1)
            if tg + 2 < 4:
                b2_norm_a(tg + 2)
            b2_pw2(tg)
    barrier()
    esB2.close()
    esB.close()

    if stop_after == "b2":
        d0 = sb("d0", [128, 16384], F32)
        for c in range(KC):
            do(DVE, lambda e: e.tensor_copy(out=d0[:, c * 2048:(c + 1) * 2048], in_=cm[:, c, :]), reads=cmR[c])
        return finish(d0[:, :], 16384)

    esA = ExitStack()

    def sbA(name, shape, dtype):
        return esA.enter_context(nc.sbuf_tensor(name, shape, dtype))

    wqkv = [sbA("wqkv%d" % i, [128, KC, 384], BF16) for i in range(2)]
    wqkvR = [Res(), Res()]
    dq_wqkv = [Dq(nc, "wqkv%d" % i) for i in range(2)]
    wg = sbA("wg", [128, KC, 256], BF16)
    wgR = Res()
    dq_wg = Dq(nc, "wg")
    KT = [sbA("KT%d" % i, [128, SEQ], BF16) for i in range(2)]
    KTR = [[Res() for _ in range(NB)] for _ in range(2)]
    QT = [sbA("QT%d" % i, [128, TOK], BF16) for i in range(2)]
    QTR = [[Res() for _ in range(NO)] for _ in range(2)]
    VX = [sbA("VX%d" % i, [128, NB, 129], BF16) for i in range(2)]
    VXR = [[Res() for _ in range(NB)] for _ in range(2)]
    attT = sbA("attT", [128, TOK], BF16)
    attTR = [Res() for _ in range(NO)]
    PT = [sbA("PT%d" % i, [128, 2, 512], BF16) for i in range(3)]
    PTR = [Res() for _ in range(3)]
    PTd = [sbA("PTd%d" % i, [128, 2, 256], BF16) for i in range(2)]
    PTdR = [Res() for _ in range(2)]
    NRB = 4
    ra = [sbA("ra%d" % i, [128, 256], F32) for i in range(NRB)]
    rb = [sbA("rb%d" % i, [128, 256], F32) for i in range(NRB)]
    rabR = [Res() for _ in range(NRB)]
    qk_tm = [sbA("qk_tm%d" % i, [128, 256], BF16) for i in range(NRB)]
    qkR = [Res() for _ in range(NRB)]
    Oc = [sbA("Oc%d" % i, [128, 2, 132], F32) for i in range(2)]
    OcR = [Res(), Res()]
    nst = [sbA("nst%d" % i, [128, 8], F32) for i in range(2)]
    nstR = [Res(), Res()]
    dd = [sbA("dd%d" % i, [128, 128], F32) for i in range(2)]
    ddR = [Res(), Res()]
    A16 = sbA("A16", [128, NO, 128], F32)
    A16R = [Res() for _ in range(NO)]
    ss16 = sbA("ss16", [128, 3, NO], F32)
    ss16R = Res()
    junk = sbA("junk", [128, 128], F32)
    att16 = sbA("att16", [128, NO, 128], BF16)
    att16R = [Res() for _ in range(NO)]
    sA = sbA("sA", [128, 512], F32)
    sC = sbA("sC", [128, 512], F32)
    sAR = Res()
    sCR = Res()
    misc_bf = P8[:, 7, :].bitcast(BF16)

    for i in range(2):
        do(POOL, lambda e, i=i: e.memset(VX[i][:, :, 128:129], 1.0), writes=VXR[i])
        do(POOL, lambda e, i=i: e.memset(PTd[i][:, :, :], 0.0), writes=[PTdR[i]])

    def load_wqkv(h):
        hb = h % 2
        for i, base in enumerate((0, 1024, 2048)):
            t = dma(POOL, dq_wqkv[hb], wqkv[hb][:, :, i * 128:(i + 1) * 128],
                    w_in[:, base + h * 128:base + (h + 1) * 128].rearrange("(k p) n -> p k n", p=128),
                    writes=[wqkvR[hb]] if i == 0 else [])
        wqkvR[hb].w = [t]

    def load_wg(h):
        for i, base in enumerate((5120, 6144)):
            t = dma(POOL, dq_wg, wg[:, :, i * 128:(i + 1) * 128],
                    w_in[:, base + h * 128:base + (h + 1) * 128].rearrange("(k p) n -> p k n", p=128),
                    writes=[wgR] if i == 0 else [])
        wgR.w = [t]

    pcount = [0]
    proj_pending = []

    def flush_proj(keep):
        while len(proj_pending) > keep:
            proj_pending.pop(0)()

    def proj_block(h, tb, pbank=6):
        hb = h % 2
        w = wqkv[hb]
        own = (tb % 2 == 1)
        j = tb // 2
        c0 = 0 if own else 128
        pb = pcount[0] % NRB
        pcount[0] += 1
        pps = P8[:, pbank, :]
        group(PE, [lambda e, k=k: e.matmul(pps[:, c0:384], xT_bf[:, k, tb * 128:(tb + 1) * 128], w[:, k, c0:384],
                                           start=(k == 0), stop=(k == KC - 1)) for k in range(KC)],
              reads=[xTR, wqkvR[hb]], writes=[bankR[pbank]])
        do(DVE, lambda e: e.tensor_copy(out=VX[hb][:, tb, 0:128], in_=pps[:, 256:384]),
           reads=[bankR[pbank]], writes=[VXR[hb][tb]])
        nco = 4 if own else 2
        cs = slice(0, 4) if own else slice(2, 4)
        T = pps[:, c0:256].rearrange("p (c t d) -> p c t d", c=nco, t=2)
        cosb = cosT[:, tb, :].unsqueeze(1).unsqueeze(1).to_broadcast([128, nco, 2, 32])
        sinb = sinT[:, tb, :].unsqueeze(1).unsqueeze(1).to_broadcast([128, nco, 2, 32])
        RA = ra[pb][:, :].rearrange("p (c t d) -> p c t d", c=4, t=2)
        RB = rb[pb][:, :].rearrange("p (c t d) -> p c t d", c=4, t=2)
        QK = qk_tm[pb][:, :].rearrange("p (c t d) -> p c t d", c=4, t=2)
        do(DVE, lambda e: e.tensor_tensor(out=RA[:, cs], in0=T, in1=cosb, op=ALU.mult),
           reads=[bankR[pbank], tabR], writes=[rabR[pb]])
        do(DVE, lambda e: e.tensor_tensor(out=RB[:, cs], in0=T, in1=sinb, op=ALU.mult),
           reads=[bankR[pbank], tabR], writes=[rabR[pb]])
        do(POOL, lambda e: e.tensor_tensor(out=QK[:, cs, 0, :], in0=RA[:, cs, 0, :], in1=RB[:, cs, 1, :], op=ALU.subtract),
           reads=[rabR[pb]], writes=[qkR[pb]])
        do(POOL, lambda e: e.tensor_tensor(out=QK[:, cs, 1, :], in0=RB[:, cs, 0, :], in1=RA[:, cs, 1, :], op=ALU.add),
           reads=[rabR[pb]], writes=[qkR[pb]])
        def finish_block(att_j=None):
            fns = [lambda e: e.transpose(misc_bf[:, 0:128], qk_tm[pb][:, 128:256], ident_b[:, :])]
            rds = [qkR[pb], constR]
            if own:
                fns.append(lambda e: e.transpose(misc_bf[:, 128:256], qk_tm[pb][:, 0:128], ident_b[:, :]))
            if att_j is not None:
                fns.append(lambda e: e.transpose(misc_bf[:, 256:384], att16[:, att_j, :], ident_b[:, :]))
                rds.append(att16R[att_j])
            group(PE, fns, reads=rds, writes=[bankR[7]])
            do(DVE, lambda e: e.tensor_copy(out=KT[hb][:, tb * 128:(tb + 1) * 128], in_=misc_bf[:, 0:128]),
               reads=[bankR[7]], writes=[KTR[hb][tb]])
            if own:
                do(DVE, lambda e: e.tensor_copy(out=QT[hb][:, j * 128:(j + 1) * 128], in_=misc_bf[:, 128:256]),
                   reads=[bankR[7]], writes=[QTR[hb][j]])
            if att_j is not None:
                do(DVE, lambda e: e.tensor_copy(out=attT[:, att_j * 128:(att_j + 1) * 128], in_=misc_bf[:, 256:384]),
                   reads=[bankR[7]], writes=[attTR[att_j]])
        proj_pending.append(finish_block)

    class Batch:
        pass

    def make_batches(h):
        out = []
        for j in range(max_j):
            regs = list(range(2 * j))
            bl = []
            while regs:
                bl.append(("r", regs[:4]))
                regs = regs[4:]
            bl.append(("s", [2 * j, 2 * j + 1]))
            for i, (kind, slots) in enumerate(bl):
                b = Batch()
                b.h, b.j, b.kind, b.slots = h, j, kind, slots
                b.idx, b.nb = i, len(bl)
                b.first = (i == 0)
                b.last = (i == len(bl) - 1)
                out.append(b)
        return out

    cnt = {"st": 0, "pt": 0, "ptd": 0, "n": 0}

    def qk(b):
        hb = b.h % 2
        b.buf = cnt["st"] % 2
        cnt["st"] += 1
        fns = []
        for c in range(2):
            for i, s_ in enumerate(b.slots):
                fns.append(lambda e, c=c, i=i, s_=s_: e.matmul(
                    P8[:, 2 * b.buf + c, i * 128:(i + 1) * 128],
                    KT[hb][64 * c:64 * c + 64, s_ * 128:(s_ + 1) * 128],
                    QT[hb][64 * c:64 * c + 64, b.j * 128:(b.j + 1) * 128], start=True, stop=True))
        group(PE, fns, reads=[KTR[hb][s_] for s_ in b.slots] + [QTR[hb][b.j]],
              writes=[bankR[2 * b.buf], bankR[2 * b.buf + 1]])

    def ex(b):
        bk = [bankR[2 * b.buf], bankR[2 * b.buf + 1]]
        n = len(b.slots)
        stv = P8[:, 2 * b.buf:2 * b.buf + 2, :]
        if b.kind == "r":
            b.pi = cnt["pt"] % 3
            cnt["pt"] += 1
            pt = PT[b.pi]
            b.pt, b.ptR = pt, PTR[b.pi]
            do(ACT, lambda e: e.activation(out=pt[:, :, 0:n * 128], in_=stv[:, :, 0:n * 128], func=AF.Exp,
                                           bias=cst[:, 0:1], scale=0.125), reads=bk + [constR], writes=[b.ptR])
        else:
            b.pi = cnt["ptd"] % 2
            cnt["ptd"] += 1
            pt = PTd[b.pi]
            b.pt, b.ptR = pt, PTdR[b.pi]
            do(ACT, lambda e: e.activation(out=pt[:, :, 0:128], in_=stv[:, :, 0:128], func=AF.Exp,
                                           bias=mb[:, b.j:b.j + 1], scale=0.125), reads=bk + [constR], writes=[b.ptR])
            do(ACT, lambda e: e.activation(out=pt[0:64, :, 128:256], in_=stv[0:64, :, 128:256], func=AF.Exp,
                                           bias=cst[0:64, 0:1], scale=0.125), reads=bk, writes=[b.ptR])
            do(ACT, lambda e: e.activation(out=pt[64:128, :, 192:256], in_=stv[64:128, :, 192:256], func=AF.Exp,
                                           bias=cst[64:128, 0:1], scale=0.125), reads=bk, writes=[b.ptR])

    def pvmm(b):
        hb = b.h % 2
        n = len(b.slots)
        if b.first:
            cnt["ob"] = cnt.get("ob", -1) + 1
        ob = cnt["ob"] % 2
        b.ob = ob
        fns = []
        for c in range(2):
            for i, s_ in enumerate(b.slots):
                fns.append(lambda e, c=c, i=i, s_=s_: e.matmul(
                    P8[:, 4 + ob, c * 129:(c + 1) * 129], b.pt[:, c, i * 128:(i + 1) * 128], VX[hb][:, s_, 0:129],
                    start=(b.first and c == 0 and i == 0), stop=False, skip_group_check=True))
        group(PE, fns, reads=[b.ptR] + [VXR[hb][s_] for s_ in b.slots], writes=[bankR[4 + ob]])

    def normalize(b):
        ob = cnt["n"] % 2
        cnt["n"] += 1
        j = b.j
        oc, st_, d_ = Oc[ob], nst[ob], dd[ob]
        obank = P8[:, 4 + b.ob, 0:258]
        do(DVE, lambda e: e.tensor_copy(out=oc[:, :, 0:129], in_=obank.rearrange("p (c v) -> p c v", c=2)),
           reads=[bankR[4 + b.ob]], writes=[OcR[ob]])
        do(DVE, lambda e: e.reciprocal(out=st_[:, 0:2], in_=oc[:, :, 128]), reads=[OcR[ob]], writes=[nstR[ob]])
        do(DVE, lambda e: e.tensor_tensor(out=st_[:, 2:3], in0=st_[:, 1:2], in1=lamc[:, 5:6], op=ALU.mult),
           reads=[nstR[ob], lamR], writes=[nstR[ob]])
        do(DVE, lambda e: e.tensor_scalar(out=d_[:, :], in0=oc[:, 0, 0:128], scalar1=st_[:, 0:1], scalar2=None,
                                          op0=ALU.mult), reads=[OcR[ob], nstR[ob]], writes=[ddR[ob]])
        do(DVE, lambda e: e.scalar_tensor_tensor(out=A16[:, j, :], in0=oc[:, 1, 0:128], scalar=st_[:, 2:3], in1=d_[:, :],
                                                 op0=ALU.mult, op1=ALU.add), reads=[OcR[ob], nstR[ob], ddR[ob]],
           writes=[A16R[j]])
        do(DVE, lambda e: e.scalar_tensor_tensor(out=junk[:, :], in0=A16[:, j, :], scalar=1.0, in1=A16[:, j, :],
                                                 op0=ALU.mult, op1=ALU.mult, accum_out=ss16[:, 0, j:j + 1]),
           reads=[A16R[j]], writes=[ss16R])

    def finalize_head():
        do(ACT, lambda e: e.activation(out=ss16[:, 1, :], in_=ss16[:, 0, :], func=AF.Sqrt, bias=cst[:, 2:3],
                                       scale=1.0 / 128.0), reads=[ss16R, constR], writes=[ss16R])
        do(DVE, lambda e: e.reciprocal(out=ss16[:, 2, :], in_=ss16[:, 1, :]), reads=[ss16R], writes=[ss16R])
        for j in range(NO):
            do(DVE, lambda e, j=j: e.scalar_tensor_tensor(out=att16[:, j, :], in0=A16[:, j, :], scalar=ss16[:, 2, j:j + 1],
                                                          in1=sg08[:, :], op0=ALU.mult, op1=ALU.mult),
               reads=[A16R[j], ss16R, constR], writes=[att16R[j]])

    def att_transpose(j):
        group(PE, [lambda e: e.transpose(misc_bf[:, 256:384], att16[:, j, :], ident_b[:, :])],
              reads=[att16R[j], constR], writes=[bankR[7]])
        do(DVE, lambda e: e.tensor_copy(out=attT[:, j * 128:(j + 1) * 128], in_=misc_bf[:, 256:384]),
           reads=[bankR[7]], writes=[attTR[j]])

    def gates_merge(h):
        for tg in range(4):
            bA = 2 * (tg % 2)
            bC = bA + 1
            sl = slice(tg * 512, (tg + 1) * 512)
            rhs = [xT_blk[k][:, 8 * tg + 1:8 * tg + 8:2, :] for k in range(KC)]
            for bnk, off in ((bA, 0), (bC, 128)):
                group(PE, [lambda e, k=k: e.matmul(P8[:, bnk, :].rearrange("p (b t) -> p b t", t=128),
                                                   wg[:, k, off:off + 128], rhs[k], start=(k == 0), stop=(k == KC - 1))
                           for k in range(KC)], reads=[xTR, wgR], writes=[bankR[bnk]])
            do(ACT, lambda e: e.activation(out=sA[:, :], in_=P8[:, bA, :], func=AF.Sigmoid, bias=pv[:, 16 + h:17 + h],
                                           scale=1.0), reads=[bankR[bA]], writes=[sAR])
            do(ACT, lambda e: e.activation(out=sC[:, :], in_=P8[:, bC, :], func=AF.Sigmoid, bias=pv[:, 24 + h:25 + h],
                                           scale=1.0), reads=[bankR[bC]], writes=[sCR])
            do(DVE, lambda e: e.tensor_tensor(out=sA[:, :], in0=sA[:, :], in1=attT[:, sl], op=ALU.mult),
               reads=attTR[4 * tg:4 * tg + 4], writes=[sAR])
            do(DVE, lambda e: e.tensor_tensor(out=sC[:, :], in0=sC[:, :], in1=cm[:, h, sl], op=ALU.mult),
               reads=[cmR[h][tg]], writes=[sCR])
            do(DVE, lambda e: e.tensor_tensor(out=cm[:, h, sl], in0=sA[:, :], in1=sC[:, :], op=ALU.add),
               reads=[sAR, sCR], writes=[cmR[h][tg]])

    NHEADS = H if stop_after not in ("a1", "aq") else 1
    load_wqkv(0)
    if NHEADS > 1:
        load_wqkv(1)
    load_wg(0)
    for tb in range(NB):
        proj_block(0, tb, pbank=(6 if tb % 2 == 0 else 0))
        flush_proj(3)
    flush_proj(0)
    for h in range(NHEADS):
        if h + 2 < NHEADS:
            load_wqkv(h + 2)
        batches = make_batches(h)
        first_pending = False
        def warm_burst():
            group(PE, [lambda e: e.matmul(P8[:, 6, 0:128], xT_bf[:, 0, 0:128], xT_bf[:, 0, 0:128],
                                          start=True, stop=True) for _ in range(36)],
                  reads=[xTR], writes=[bankR[6]])
        last_head = (h == NHEADS - 1 and NHEADS > 1)
        qk(batches[0])
        for i, b in enumerate(batches):
            ex(b)
            if i + 1 < len(batches):
                qk(batches[i + 1])
            pvmm(b)
            if b.first:
                first_pending = True
            if first_pending and (b.last or b.idx >= max(1, b.nb // 2)):
                first_pending = False
                if h + 1 < NHEADS:
                    flush_proj(2)
                    proj_block(h + 1, 2 * b.j)
            if b.last and last_head and b.j in (1, 4, 8, 12):
                warm_burst()
            if b.last:
                att_done = False
                if h + 1 < NHEADS:
                    while len(proj_pending) > 2:
                        fb = proj_pending.pop(0)
                        if h > 0 and not att_done:
                            fb(att_j=b.j)
                            att_done = True
                        else:
                            fb()
                if h + 1 < NHEADS:
                    proj_block(h + 1, 2 * b.j + 1)
                normalize(b)
                if h > 0 and not att_done:
                    att_transpose(b.j)
        flush_proj(0)
        if h > 0:
            gates_merge(h - 1)
            load_wg(h)
        finalize_head()
    for j in range(max_j):
        att_transpose(j)
    if stop_after == "aq":
        barrier()
        dq = Dq(nc, "dbg")
        t = dma(POOL, dq, dbg[:, 0:2048], attT[:, :])
        POOL.wait([t])
        return nc
    gates_merge(NHEADS - 1)
    barrier()
    esA.close()

    if stop_after in ("a", "a1"):
        d0 = sb("d0", [128, 16384], F32)
        for c in range(KC):
            do(DVE, lambda e: e.tensor_copy(out=d0[:, c * 2048:(c + 1) * 2048], in_=cm[:, c, :]), reads=cmR[c])
        return finish(d0[:, :], 16384)

    esX.close()
    esT = ExitStack()

    def sbT(name, shape, dtype):
        return esT.enter_context(nc.sbuf_tensor(name, shape, dtype))

    pr = sbT("pr", [128, 4096], F32)
    dq_pr = Dq(nc, "pr")
    t_pr = dma(SP, dq_pr, pr[:, :], prow[0:1, 128:4224].broadcast_to([128, 4096]))
    prR = Res()
    prR.w = [t_pr]
    Wout = sbT("Wout", [128, KC, D], BF16)
    WoutR = Res()
    dq_wout = Dq(nc, "wout")
    WoutRh = [Res(), Res()]
    dq_wout2 = Dq(nc, "wout2")
    for hf, dqw in ((0, dq_wout), (1, dq_wout2)):
        t_w = dma(POOL, dqw, Wout[:, :, hf * 512:(hf + 1) * 512],
                  w_out[:, hf * 512:(hf + 1) * 512].rearrange("(k p) n -> p k n", p=128))
        WoutRh[hf].w = [t_w]
    WoutR.w = WoutRh[0].w + WoutRh[1].w
    NW1, NW2 = 4, 4
    W1 = [sbT("W1_%d" % i, [128, KC, 256], BF16) for i in range(NW1)]
    W1R = [Res() for _ in range(NW1)]
    dq_w1 = [Dq(nc, "w1_%d" % i) for i in range(NW1)]
    W2 = [sbT("W2_%d" % i, [128, 4, 512], BF16) for i in range(NW2)]
    W2R = [Res() for _ in range(NW2)]
    dq_w2 = [Dq(nc, "w2_%d" % i) for i in range(NW2)]
    xo = [sbT("xo%d" % i, [128, D], F32) for i in range(2)]
    xoR = [Res() for _ in range(2)]
    dq_xo = [Dq(nc, "xo%d" % i) for i in range(2)]
    r2 = sbT("r2", [128, 4, D], F32)
    r2R = [Res() for _ in range(4)]
    dq_y = [Dq(nc, "y%d" % i) for i in range(4)]
    h1 = [sbT("h1_%d" % i, [128, 4, D], F32) for i in range(2)]
    h1R = [[Res() for _ in range(4)] for _ in range(2)]
    h1b = sbT("h1b", [128, 4, D], BF16)
    h1bR = [Res() for _ in range(4)]
    h1T = sbT("h1T", [128, KC, 512], BF16)
    h1TR = Res()
    hr = [sbT("hr%d" % i, [128, 512], F32) for i in range(2)]
    hrR = [Res(), Res()]
    hidT = sbT("hidT", [128, 32, 512], BF16)
    hidR = [Res() for _ in range(32)]
    bst = [sbT("bst%d" % i, [128, 2, 6], F32) for i in range(2)]
    bmv = [sbT("bmv%d" % i, [128, 4], F32) for i in range(2)]
    bstR = [Res(), Res()]
    lcount = [0]

    def layer_norm(src, dst, g_ap, b_ap, srcR, dstR):
        i = lcount[0] % 2
        lcount[0] += 1
        st, mv = bst[i], bmv[i]
        do(DVE, lambda e: e.bn_stats(out=st[:, 0, :], in_=src[:, 0:512]), reads=[srcR], writes=[bstR[i]])
        do(DVE, lambda e: e.bn_stats(out=st[:, 1, :], in_=src[:, 512:1024]), reads=[srcR], writes=[bstR[i]])
        do(DVE, lambda e: e.bn_aggr(out=mv[:, 0:2], in_=st[:, :, :].rearrange("p a b -> p (a b)")),
           reads=[bstR[i]], writes=[bstR[i]])
        do(ACT, lambda e: e.activation(out=mv[:, 2:3], in_=mv[:, 1:2], func=AF.Sqrt, bias=cst[:, 2:3], scale=1.0),
           reads=[bstR[i], constR], writes=[bstR[i]])
        do(DVE, lambda e: e.reciprocal(out=mv[:, 3:4], in_=mv[:, 2:3]), reads=[bstR[i]], writes=[bstR[i]])
        rr = [srcR, bstR[i]] if srcR is not dstR else [bstR[i]]
        do(DVE, lambda e: e.tensor_scalar(out=dst, in0=src, scalar1=mv[:, 0:1], scalar2=mv[:, 3:4],
                                          op0=ALU.subtract, op1=ALU.mult), reads=rr, writes=[dstR])
        do(DVE, lambda e: e.tensor_tensor(out=dst, in0=dst, in1=g_ap, op=ALU.mult), reads=[prR], writes=[dstR])
        do(DVE, lambda e: e.tensor_tensor(out=dst, in0=dst, in1=b_ap, op=ALU.add), reads=[prR], writes=[dstR])

    w1_issued = [0]
    w2_issued = [0]

    def issue_w1(n):
        while w1_issued[0] < n and w1_issued[0] < 4 * 16:
            q = w1_issued[0]
            g = q % 16
            i = q % NW1
            dma(POOL, dq_w1[i], W1[i][:, :, :], w_ff1[:, g * 256:(g + 1) * 256].rearrange("(k p) n -> p k n", p=128),
                writes=[W1R[i]])
            w1_issued[0] += 1

    def issue_w2(n):
        while w2_issued[0] < n and w2_issued[0] < 4 * 16:
            q = w2_issued[0]
            hf, g4 = (q % 16) // 8, q % 8
            i = q % NW2
            dma(POOL, dq_w2[i], W2[i][:, :, :],
                w_ff2[g4 * 512:(g4 + 1) * 512, hf * 512:(hf + 1) * 512].rearrange("(fl p) n -> p fl n", p=128),
                writes=[W2R[i]])
            w2_issued[0] += 1

    def t1_block_mm(tg, tbl):
        hp = tg % 2
        tb = 4 * tg + tbl
        xs = tbl % 2
        dma(SP, dq_xo[xs], xo[xs][:, :], xown[tb * 128:(tb + 1) * 128, :], writes=[xoR[xs]])
        for hf in range(2):
            hs = slice(hf * 512, (hf + 1) * 512)
            mbk = 4 + (2 * tbl + hf) % 4
            group(PE, [lambda e, c=c: e.matmul(P8[:, mbk, :], cm[:, c, tb * 128:(tb + 1) * 128], Wout[:, c, hs],
                                               start=(c == 0), stop=(c == KC - 1)) for c in range(KC)],
                  reads=[cmR[c][tg] for c in range(KC)] + [WoutRh[hf]], writes=[bankR[mbk]])
            do(DVE, lambda e: e.scalar_tensor_tensor(out=h1[hp][:, tbl, hs], in0=xo[xs][:, hs], scalar=ALPHA,
                                                     in1=P8[:, mbk, :], op0=ALU.mult, op1=ALU.add),
               reads=[xoR[xs], bankR[mbk]], writes=[h1R[hp][tbl]])

    def t1_block_ln(tg, tbl):
        hp = tg % 2
        layer_norm(h1[hp][:, tbl, :], h1[hp][:, tbl, :], pr[:, 0:1024], pr[:, 1024:2048], h1R[hp][tbl], h1R[hp][tbl])
        do(DVE, lambda e: e.tensor_copy(out=h1b[:, tbl, :], in_=h1[hp][:, tbl, :]),
           reads=[h1R[hp][tbl]], writes=[h1bR[tbl]])

    def t1_matmuls(tg):
        for tbl in range(4):
            t1_block_mm(tg, tbl)

    def t1_ln(tg):
        for tbl in range(4):
            t1_block_ln(tg, tbl)

    def t1_transposes(tg):
        for tbl in range(4):
            trb = 6 + (tbl % 2)
            trv = P8[:, trb, :].bitcast(BF16)
            group(PE, [lambda e, c=c: e.transpose(trv[:, c * 128:(c + 1) * 128], h1b[:, tbl, c * 128:(c + 1) * 128],
                                                  ident_b[:, :]) for c in range(KC)],
                  reads=[h1bR[tbl], constR], writes=[bankR[trb]])
            do(ACT, lambda e: e.copy(out=h1T[:, :, tbl * 128:(tbl + 1) * 128],
                                     in_=trv.rearrange("p (c t) -> p c t", t=128)),
               reads=[bankR[trb]], writes=[h1TR])

    def ff1(tg):
        for f in range(32):
            g, fl = f // 2, f % 2
            q = tg * 16 + g
            wi = q % NW1
            hb_ = 4 + (f % 2)
            group(PE, [lambda e, c=c: e.matmul(P8[:, hb_, :], W1[wi][:, c, fl * 128:(fl + 1) * 128], h1T[:, c, :],
                                               start=(c == 0), stop=(c == KC - 1)) for c in range(KC)],
                  reads=[W1R[wi], h1TR], writes=[bankR[hb_]])
            if fl == 1:
                issue_w1(q + 1 + NW1)
            do(ACT, lambda e: e.activation(out=hr[f % 2][:, :], in_=P8[:, hb_, :], func=AF.Relu, bias=cst[:, 0:1],
                                           scale=1.0), reads=[bankR[hb_], constR], writes=[hrR[f % 2]])
            do(DVE, lambda e: e.tensor_tensor(out=hidT[:, f, :], in0=hr[f % 2][:, :], in1=hr[f % 2][:, :], op=ALU.mult),
               reads=[hrR[f % 2]], writes=[hidR[f]])

    def ff2(tg, hf):
        hp = tg % 2
        hs = slice(hf * 512, (hf + 1) * 512)
        for g4 in range(8):
            q = tg * 16 + hf * 8 + g4
            wi = q % NW2
            fns = []
            for tbl in range(4):
                for fl in range(4):
                    f = 4 * g4 + fl
                    fns.append(lambda e, tbl=tbl, fl=fl, f=f: e.matmul(
                        P8[:, tbl, :], hidT[:, f, tbl * 128:(tbl + 1) * 128], W2[wi][:, fl, :],
                        start=(g4 == 0 and fl == 0), stop=(g4 == 7 and fl == 3)))
            group(PE, fns, reads=[W2R[wi]] + [hidR[4 * g4 + fl] for fl in range(4)],
                  writes=[bankR[0], bankR[1], bankR[2], bankR[3]])
            issue_w2(q + 1 + NW2)
        for tbl in range(4):
            do(DVE, lambda e: e.scalar_tensor_tensor(out=r2[:, tbl, hs], in0=h1[hp][:, tbl, hs], scalar=ALPHA,
                                                     in1=P8[:, tbl, :], op0=ALU.mult, op1=ALU.add),
               reads=[h1R[hp][tbl], bankR[tbl]], writes=[r2R[tbl]])

    def ln2_store(tg):
        for tbl in range(4):
            tb = 4 * tg + tbl
            layer_norm(r2[:, tbl, :], r2[:, tbl, :], pr[:, 2048:3072], pr[:, 3072:4096], r2R[tbl], r2R[tbl])
            dma(SP, dq_y[tbl], y[tb * 128:(tb + 1) * 128, :], r2[:, tbl, :], reads=[r2R[tbl]])

    issue_w1(NW1)
    issue_w2(NW2)
    for tbl in range(4):
        t1_block_mm(0, tbl)
        if tbl >= 1:
            t1_block_ln(0, tbl - 1)
    t1_block_ln(0, 3)
    t1_transposes(0)
    for tg in range(4):
        ff1(tg)
        if tg > 0:
            ln2_store(tg - 1)
        ff2(tg, 0)
        if tg + 1 < 4:
            t1_matmuls(tg + 1)
            t1_ln(tg + 1)
        ff2(tg, 1)
        if tg + 1 < 4:
            t1_transposes(tg + 1)
    ln2_store(3)
    SP.wait([(q_, q_.cnt) for q_ in dq_y])
    barrier()
    esT.close()
    return nc


def ctx_order(half):
    if half == 1:
        return list(range(NB))
    o = []
    for i in range(0, NB, 2):
        o += [i + 1, i]
    return o


def make_in_maps(inp):
    x = np.asarray(inp["x"], dtype=np.float32)
    positions = np.asarray(inp["positions"]).astype(np.int32)
    f = lambda k: np.ascontiguousarray(np.asarray(inp[k], dtype=np.float32)[0])
    w_in, w_pw2, w_out, w_ff1, w_ff2 = f("w_in"), f("w_pw2"), f("w_out"), f("w_ff1"), f("w_ff2")
    col = lambda v, n: np.asarray(v, dtype=np.float32).reshape(n, 128).T
    dwk = np.asarray(inp["dw_kernel"], dtype=np.float32)[0]
    dwk_cols = dwk.reshape(31, 8, 128).transpose(2, 1, 0).reshape(128, 248)
    pvec = np.concatenate([
        col(inp["b_glu"][0], 16), col(inp["b_gate"][0], 16), col(inp["dw_bias"][0], 8),
        col(inp["conv_ln_g"][0], 8), col(inp["conv_ln_b"][0], 8), col(inp["b_pw2"][0], 8),
        dwk_cols], axis=1).astype(np.float32)
    assert pvec.shape == (128, NPV)
    prow = np.concatenate([
        np.asarray(inp[k], dtype=np.float32).reshape(-1) for k in
        ["subln_g", "ln1_g", "ln1_b", "ln2_g", "ln2_b", "lambda_q1", "lambda_k1", "lambda_q2", "lambda_k2"]
    ]).reshape(1, NPR).astype(np.float32)
    ident = np.eye(128, dtype=np.float32)
    maps = []
    for core in range(8):
        b, half = core // 2, core % 2
        order = ctx_order(half)
        tok_idx = np.concatenate([np.arange(g * 128, (g + 1) * 128) for g in order])
        xb = x[b]
        xT = np.ascontiguousarray(xb[tok_idx].T)
        own_blocks = [2 * j + half for j in range(NO)]
        own_idx = np.concatenate([np.arange(g * 128, (g + 1) * 128) for g in own_blocks])
        xown = np.ascontiguousarray(xb[own_idx])
        xh = np.zeros((NO * 32, D), dtype=np.float32)
        for j, g in enumerate(own_blocks):
            if g > 0:
                xh[j * 32:(j + 1) * 32] = xb[g * 128 - 32:g * 128]
        xhT = np.ascontiguousarray(xh.T)
        pos_ctx = positions[b][tok_idx]
        pos_tm = np.ascontiguousarray(pos_ctx.reshape(NB, 128).T).astype(np.int32)
        pos_blk = np.ascontiguousarray(pos_ctx.reshape(NB, 128)[:, 0].reshape(1, NB)).astype(np.int32)
        maps.append(dict(xT=xT, xhT=xhT, xown=xown, pos_tm=pos_tm, pos_blk=pos_blk, w_in=w_in, w_pw2=w_pw2,
                         w_out=w_out, w_ff1=w_ff1, w_ff2=w_ff2, pvec=pvec, prow=prow, ident=ident))
    return maps


def kernel(**inputs):
    maps = make_in_maps(inputs)
    nc = build_program()
    res = run_bass_kernel_spmd(nc, maps, core_ids=list(range(8)))
    out = np.zeros((4, SEQ, D), dtype=np.float32)
    for core in range(8):
        b, half = core // 2, core % 2
        yc = np.asarray(res.results[core]["y"], dtype=np.float32)
        for j in range(NO):
            g = 2 * j + half
            out[b, g * 128:(g + 1) * 128] = yc[j * 128:(j + 1) * 128]
    return out
```
